# Optimizing a Trainium2 kernel written in Bass

```python
import math
import jax, jax.numpy as jnp
from jax import lax
import numpy as np

D_MODEL = 1024
BATCH = 16
SEQ = 4096
DEPTH = 2

CTX_LEN = 256
GRID_W = 64
MIX_WIDTH = D_MODEL
GROUP_W = MIX_WIDTH // 4
N_SUB = 4
SUB_W = GROUP_W // N_SUB
CONV_W = 3
POOL_WINDOWS = (2, 4, 8, 16)
MLA_HEADS = 4
MLA_NOPE = GROUP_W // MLA_HEADS
MLA_ROPE = MLA_NOPE // 2
MLA_V = GROUP_W // MLA_HEADS
MLA_Q_RANK = GROUP_W
MLA_KV_RANK = GROUP_W // 2
MLA_SCALE = 1.0 / math.sqrt(MLA_NOPE + MLA_ROPE)
ROPE_BASE = 10000.0
D_FF = 4 * D_MODEL
EPS = 1e-6
Q_BLOCK = 128

OFF_FOURIER = 0
OFF_CONV = OFF_FOURIER + GROUP_W
OFF_POOL = OFF_CONV + 3 * GROUP_W
OFF_MLA_Q = OFF_POOL + GROUP_W
OFF_MLA_KV = OFF_MLA_Q + MLA_Q_RANK
OFF_MLA_KR = OFF_MLA_KV + MLA_KV_RANK
IN_COLS = OFF_MLA_KR + MLA_ROPE

kernel_name = "hybrid_fourier_conv_pool_mla_dit"


def rmsnorm(x, g):
    xf = x.astype(jnp.float32)
    y = xf * lax.rsqrt(jnp.mean(xf * xf, axis=-1, keepdims=True) + EPS)
    return (y * g.astype(jnp.float32)).astype(x.dtype)


def modulate(h, shift, scale):
    return h * (1.0 + scale) + shift


def axial_rope_tables(n):
    rows = n // GRID_W
    row = jnp.repeat(jnp.arange(rows, dtype=jnp.float32), GRID_W)
    col = jnp.tile(jnp.arange(GRID_W, dtype=jnp.float32), rows)
    half = MLA_ROPE // 2
    inv = ROPE_BASE ** (-jnp.arange(0, half, 2, dtype=jnp.float32) / half)
    ang_r = row[:, None] * inv[None, :]
    ang_c = col[:, None] * inv[None, :]
    ang = jnp.concatenate([ang_r, ang_r, ang_c, ang_c], axis=-1)
    return jnp.cos(ang), jnp.sin(ang)


def apply_axial_rope(x, cos, sin):
    half = MLA_ROPE // 2
    quarter = half // 2

    def rot(v):
        return jnp.concatenate([-v[..., quarter:], v[..., :quarter]], axis=-1)

    rotated = jnp.concatenate([rot(x[..., :half]), rot(x[..., half:])], axis=-1)
    return x * cos.astype(x.dtype) + rotated * sin.astype(x.dtype)


def fourier_mix(u, w):
    b, n, _ = u.shape
    f = jnp.fft.fft2(u.astype(jnp.float32).reshape(b, n, N_SUB, SUB_W), axes=(1, 3), norm="ortho").real
    return f.reshape(b, n, GROUP_W).astype(u.dtype) @ w


def short_conv_mix(u, conv_w):
    bg, cg, xin = jnp.split(u, 3, axis=-1)
    z = cg * xin
    n = z.shape[1]
    pad = CONV_W // 2
    zp = jnp.pad(z, ((0, 0), (pad, CONV_W - 1 - pad), (0, 0)))
    y = sum(zp[:, k:k + n] * conv_w[k] for k in range(CONV_W))
    return bg * y


def pool_mix(u, pool_w, pool_scale):
    b, n, _ = u.shape
    uf = u.astype(jnp.float32)
    cs = jnp.concatenate([jnp.zeros((b, 1, GROUP_W), jnp.float32), jnp.cumsum(uf, axis=1)], axis=1)
    t = jnp.arange(n)
    outs = []
    for g, w in enumerate(POOL_WINDOWS):
        lo = jnp.maximum(t - w // 2, 0)
        hi = jnp.minimum(t + w // 2 - 1, n - 1)
        sl = slice(g * SUB_W, (g + 1) * SUB_W)
        csg = cs[..., sl]
        s = jnp.take(csg, hi + 1, axis=1) - jnp.take(csg, lo, axis=1)
        cnt = (hi - lo + 1).astype(jnp.float32)[None, :, None]
        outs.append((s / cnt - uf[..., sl]).astype(u.dtype) @ pool_w[g])
    return jnp.concatenate(outs, axis=-1) * pool_scale


def mla_project(p, q_norm_g, w_uq, kv_norm_g, w_ukv):
    b, n, _ = p.shape
    cq = p[..., OFF_MLA_Q:OFF_MLA_KV]
    ckv = p[..., OFF_MLA_KV:OFF_MLA_KR]
    k_pe = p[..., OFF_MLA_KR:IN_COLS]
    q = (rmsnorm(cq, q_norm_g) @ w_uq).reshape(b, n, MLA_HEADS, MLA_NOPE + MLA_ROPE)
    kv = (rmsnorm(ckv, kv_norm_g) @ w_ukv).reshape(b, n, MLA_HEADS, MLA_NOPE + MLA_V)
    return q[..., :MLA_NOPE], q[..., MLA_NOPE:], kv[..., :MLA_NOPE], k_pe, kv[..., MLA_NOPE:]


def mla_attend(q_nope, q_pe, k_nope, k_pe, v):
    s = (jnp.einsum('bqhd,bkhd->bhqk', q_nope, k_nope, preferred_element_type=jnp.float32)
         + jnp.einsum('bqhr,bkr->bhqk', q_pe, k_pe, preferred_element_type=jnp.float32))
    p = jax.nn.softmax(s * MLA_SCALE, axis=-1)
    return jnp.einsum('bhqk,bkhd->bqhd', p.astype(v.dtype), v)


def blocked_mla_attend(q_nope, q_pe, k_nope, k_pe, v):
    b, n = q_nope.shape[:2]
    nb = n // Q_BLOCK
    qn = q_nope.reshape(b, nb, Q_BLOCK, MLA_HEADS, MLA_NOPE).transpose(1, 0, 2, 3, 4)
    qp = q_pe.reshape(b, nb, Q_BLOCK, MLA_HEADS, MLA_ROPE).transpose(1, 0, 2, 3, 4)
    out = lax.map(lambda qs: mla_attend(qs[0], qs[1], k_nope, k_pe, v), (qn, qp))
    return out.transpose(1, 0, 2, 3, 4).reshape(b, n, MLA_HEADS * MLA_V)


def local_mixers(p, fourier_w, conv_w, pool_w, pool_scale):
    return (fourier_mix(p[..., OFF_FOURIER:OFF_CONV], fourier_w),
            short_conv_mix(p[..., OFF_CONV:OFF_POOL], conv_w),
            pool_mix(p[..., OFF_POOL:OFF_MLA_Q], pool_w, pool_scale))


def sq_relu_mlp(h, w1, w2):
    return jnp.square(jax.nn.relu(h @ w1)) @ w2


def setup_inputs(seed: int = 0) -> dict:
    key = jax.random.key(seed)
    ks = jax.random.split(key, 24)
    f32 = jnp.float32
    nrm = lambda k, shape, s: jax.random.normal(k, shape, f32) * s
    L = DEPTH
    return {
        "x": nrm(ks[0], (BATCH, SEQ, D_MODEL), 1.0),
        "c": nrm(ks[1], (BATCH, D_MODEL), 1.0),
        "ctx": nrm(ks[2], (BATCH, CTX_LEN, D_MODEL), 1.0),
        "c_ctx": nrm(ks[3], (D_MODEL,), 1.0),
        "ada_w": nrm(ks[4], (L, D_MODEL, 6 * D_MODEL), 0.5 * D_MODEL ** -0.5),
        "ada_b": nrm(ks[5], (L, 6 * D_MODEL), 0.02),
        "norm1_g": 1.0 + nrm(ks[6], (L, D_MODEL), 0.05),
        "norm2_g": 1.0 + nrm(ks[7], (L, D_MODEL), 0.05),
        "w_in": nrm(ks[8], (L, D_MODEL, IN_COLS), D_MODEL ** -0.5),
        "fourier_w": nrm(ks[9], (L, GROUP_W, GROUP_W), GROUP_W ** -0.5),
        "conv_w": nrm(ks[10], (L, CONV_W, GROUP_W), CONV_W ** -0.5),
        "pool_w": nrm(ks[11], (L, N_SUB, SUB_W, SUB_W), SUB_W ** -0.5),
        "pool_scale": 1.0 + nrm(ks[12], (L, GROUP_W), 0.1),
        "q_norm_g": 1.0 + nrm(ks[13], (L, MLA_Q_RANK), 0.05),
        "w_uq": nrm(ks[14], (L, MLA_Q_RANK, MLA_HEADS * (MLA_NOPE + MLA_ROPE)), MLA_Q_RANK ** -0.5),
        "kv_norm_g": 1.0 + nrm(ks[15], (L, MLA_KV_RANK), 0.05),
        "w_ukv": nrm(ks[16], (L, MLA_KV_RANK, MLA_HEADS * (MLA_NOPE + MLA_V)), MLA_KV_RANK ** -0.5),
        "w_out": nrm(ks[17], (L, MIX_WIDTH, D_MODEL), MIX_WIDTH ** -0.5),
        "mlp_w1": nrm(ks[18], (L, D_MODEL, D_FF), D_MODEL ** -0.5),
        "mlp_w2": nrm(ks[19], (L, D_FF, D_MODEL), D_FF ** -0.5),
        "final_norm_g": 1.0 + nrm(ks[20], (D_MODEL,), 0.05),
    }


def reference(x, c, ctx, c_ctx, ada_w, ada_b, norm1_g, norm2_g, w_in, fourier_w, conv_w,
              pool_w, pool_scale, q_norm_g, w_uq, kv_norm_g, w_ukv, w_out, mlp_w1, mlp_w2,
              final_norm_g):
    b, n, _ = x.shape
    cos, sin = axial_rope_tables(n)
    for l in range(DEPTH):
        last = l == DEPTH - 1
        mod_x = (jax.nn.silu(c) @ ada_w[l] + ada_b[l])[:, None, :]
        mod_c = jax.nn.silu(c_ctx) @ ada_w[l] + ada_b[l]
        sh1x, sc1x, g1x, sh2x, sc2x, g2x = jnp.split(mod_x, 6, axis=-1)
        sh1c, sc1c, g1c, sh2c, sc2c, g2c = jnp.split(mod_c, 6, axis=-1)

        hx = modulate(rmsnorm(x, norm1_g[l]), sh1x, sc1x)
        hc = modulate(rmsnorm(ctx, norm1_g[l]), sh1c, sc1c)
        px = hx @ w_in[l]
        pc = hc @ w_in[l]

        qn_c, qp_c, kn_c, kp_c, v_c = mla_project(pc, q_norm_g[l], w_uq[l], kv_norm_g[l], w_ukv[l])
        qn_x, qp_x, kn_x, kp_x, v_x = mla_project(px, q_norm_g[l], w_uq[l], kv_norm_g[l], w_ukv[l])
        qp_x = apply_axial_rope(qp_x, cos[:, None, :], sin[:, None, :])
        kp_x = apply_axial_rope(kp_x, cos, sin)
        kn_all = jnp.concatenate([kn_c, kn_x], axis=1)
        kp_all = jnp.concatenate([kp_c, kp_x], axis=1)
        v_all = jnp.concatenate([v_c, v_x], axis=1)
        attn_x = blocked_mla_attend(qn_x, qp_x, kn_all, kp_all, v_all)

        f_x, s_x, p_x = local_mixers(px, fourier_w[l], conv_w[l], pool_w[l], pool_scale[l])
        out_x = jnp.concatenate([f_x, s_x, p_x, attn_x], axis=-1) @ w_out[l]
        x = x + g1x * out_x
        x = x + g2x * sq_relu_mlp(modulate(rmsnorm(x, norm2_g[l]), sh2x, sc2x), mlp_w1[l], mlp_w2[l])

        if not last:
            attn_c = mla_attend(qn_c, qp_c, kn_c, kp_c, v_c).reshape(b, -1, GROUP_W)
            f_c, s_c, p_c = local_mixers(pc, fourier_w[l], conv_w[l], pool_w[l], pool_scale[l])
            out_c = jnp.concatenate([f_c, s_c, p_c, attn_c], axis=-1) @ w_out[l]
            ctx = ctx + g1c * out_c
            ctx = ctx + g2c * sq_relu_mlp(modulate(rmsnorm(ctx, norm2_g[l]), sh2c, sc2c), mlp_w1[l], mlp_w2[l])
    return rmsnorm(x, final_norm_g)
```

```python
import math
import os
import numpy as np
import concourse.bass as bass
import concourse.mybir as mybir
from concourse.bass_utils import run_bass_kernel_spmd

F32 = mybir.dt.float32
BF16 = mybir.dt.bfloat16
AF = mybir.ActivationFunctionType
ALU = mybir.AluOpType

NCORES = 8
D = 1024
SEQ = 4096
CTXL = 256
TOK = SEQ + CTXL
DEPTH = 2
INC = 1696
EPS = 1e-6
SCALE = 1.0 / math.sqrt(96.0)
ENGS = ['pe', 'act', 'dve', 'pool', 'sp']
SAME_ENG_SYNC = True


class _Rec:
    def __init__(self):
        self.calls = []

    def __getattr__(self, name):
        def m(*a, **k):
            self.calls.append((name, a, k))
            return len(self.calls) - 1
        return m


class Prog:
    def __init__(self, nc):
        self.nc = nc
        self.streams = {e: [] for e in ENGS}
        self.res = {}
        self.waited = {e: {} for e in ENGS}
        self.chan_count = {}
        self.signal = set()

    def _dep(self, eng, tok, waits, nosame=False):
        if tok is None:
            return
        if tok[0] == 'e':
            _, e2, idx = tok
            if e2 == eng and (eng in ('pe', 'sp') or not SAME_ENG_SYNC or nosame):
                return
            key = ('e', e2)
            if self.waited[eng].get(key, -1) >= idx:
                return
            self.waited[eng][key] = idx
            waits.append(tok)
            self.signal.add((e2, idx))
        else:
            _, ch, n = tok
            key = ('d', ch)
            if self.waited[eng].get(key, 0) >= n:
                return
            self.waited[eng][key] = n
            waits.append(tok)

    def add(self, eng, fn, r=(), w=(), chan=None, ndma=1, free_psum=False):
        waits = []
        if eng in ('act', 'dve') and not free_psum:
            def _isps(k):
                return (isinstance(k, str) and k.startswith('ps')) or (isinstance(k, tuple) and k and k[0] == 'pst')
            if any(_isps(k) for k in r) or any(_isps(k) for k in w):
                w = list(w) + ['!psum']
        for k in r:
            st = self.res.get(k)
            if st:
                self._dep(eng, st[0], waits)
        for k in w:
            st = self.res.get(k)
            if st:
                ns = isinstance(k, str) and k.startswith('!')
                self._dep(eng, st[0], waits, ns)
                for t in st[1].values():
                    self._dep(eng, t, waits, ns)
        idx = len(self.streams[eng])
        if chan is not None:
            n0 = self.chan_count.get(chan, 0)
            if n0 > 0:
                self._dep(eng, ('d', chan, n0), waits)
            n1 = n0 + ndma
            self.chan_count[chan] = n1
            tok = ('d', chan, n1)
            rk = ('d', chan)
        else:
            tok = ('e', eng, idx)
            rk = ('e', eng)
        for k in r:
            self.res.setdefault(k, [None, {}])[1][rk] = tok
        for k in w:
            self.res[k] = [tok, {}]
        rec = _Rec()
        fn(rec)
        assert rec.calls
        if chan is not None:
            assert len(rec.calls) == ndma, (len(rec.calls), ndma)
        self.streams[eng].append(dict(fn=rec.calls, waits=waits, chan=chan, idx=idx, ndma=ndma))

    def barrier(self):
        toks = []
        for e in ENGS:
            if e == 'sp':
                continue
            real = [op['idx'] for op in self.streams[e] if op['fn'] is not None and op['chan'] is None]
            if real:
                toks.append(('e', e, real[-1]))
        for ch, n in self.chan_count.items():
            toks.append(('d', ch, n))
        for e in ENGS:
            waits = []
            for t in toks:
                if t[0] == 'e' and t[1] == e:
                    if e in ('pe', 'sp') or not SAME_ENG_SYNC:
                        continue
                self._dep(e, t, waits)
            self.streams[e].append(dict(fn=None, waits=waits, chan=None, idx=len(self.streams[e]), ndma=0))
        self.res = {}

    def emit(self):
        nc = self.nc
        rank = {}
        for e in ENGS:
            c = 0
            for op in self.streams[e]:
                if op['chan'] is None and (e, op['idx']) in self.signal:
                    if op['fn'] is None:
                        raise RuntimeError("signal on empty op")
                    c += 1
                    rank[(e, op['idx'])] = c
        from contextlib import ExitStack
        with ExitStack() as es:
            esem = {e: es.enter_context(nc.semaphore("s_" + e)) for e in ENGS if e != 'sp'}
            csem = {ch: es.enter_context(nc.semaphore("c_" + ch)) for ch in self.chan_count}
            block = es.enter_context(nc.Block())

            def make_body(e):
                def body(eng):
                    for op in self.streams[e]:
                        for tok in op['waits']:
                            if tok[0] == 'e':
                                eng.wait_ge(esem[tok[1]], rank[(tok[1], tok[2])])
                            else:
                                eng.wait_ge(csem[tok[1]], 16 * tok[2])
                        if op['fn'] is None:
                            continue
                        lst = [getattr(eng, nm_)(*a_, **k_) for (nm_, a_, k_) in op['fn']]
                        if op['chan'] is not None:
                            for i_ in lst:
                                i_.then_inc(csem[op['chan']], 16)
                        elif (e, op['idx']) in rank:
                            lst[-1].then_inc(esem[e], 1)
                return body
            block.tensor(make_body('pe'))
            block.scalar(make_body('act'))
            block.vector(make_body('dve'))
            block.gpsimd(make_body('pool'))
            block.sync(make_body('sp'))


class Arena:
    def __init__(self, nc, base, limit):
        self.nc = nc
        self.off = base
        self.limit = limit
        self.stack = []
        self.n = 0
        self.peak = base

    def alloc(self, name, shape, dtype):
        per = int(np.prod(shape[1:])) * (4 if dtype == F32 else 2)
        per = (per + 63) // 64 * 64
        assert self.off + per <= self.limit, f"SBUF arena overflow at {name}: {self.off}+{per}>{self.limit}"
        self.n += 1
        t = self.nc.alloc_sbuf_tensor_at(f"{name}_{self.n}", list(shape), dtype, offset=self.off)
        self.off += per
        self.peak = max(self.peak, self.off)
        return t

    def push(self):
        self.stack.append(self.off)

    def pop(self):
        self.off = self.stack.pop()


VEC_COLS = {}


def _vec_layout():
    off = 0
    def put(name, n):
        nonlocal off
        VEC_COLS[name] = off
        off += n
    put('cT', 24)
    for l in range(DEPTH):
        put(f'n1g{l}', 8)
        put(f'n2g{l}', 8)
        put(f'qng{l}', 2)
        put(f'kvng{l}', 1)
        put(f'convw{l}', 6)
        put(f'pscale{l}', 2)
    put('invw', 2)
    return off


NVEC = _vec_layout()


def _chunked(v):
    v = np.asarray(v, np.float32)
    return np.ascontiguousarray(v.reshape(-1, 128).T)


def _const_tables():
    t = {}
    t['ident'] = np.eye(128, dtype=np.float32)
    n1 = np.arange(64)[:, None].astype(np.float64)
    k1 = np.arange(64)[None, :].astype(np.float64)
    ang = 2 * np.pi * n1 * k1 / 64.0
    t['W1'] = np.concatenate([np.cos(ang), -np.sin(ang)], axis=1).astype(np.float32)
    n2 = np.arange(64).reshape(64, 1, 1).astype(np.float64)
    kk1 = np.arange(64).reshape(1, 64, 1).astype(np.float64)
    kk2 = np.arange(64).reshape(1, 1, 64).astype(np.float64)
    th = 2 * np.pi * n2 * (kk1 + 64 * kk2) / 4096.0
    cs, sn = np.cos(th), np.sin(th)
    m2 = np.zeros((2, 64, 64, 2, 64), np.float64)
    m2[0, :, :, 0, :] = cs
    m2[0, :, :, 1, :] = -sn
    m2[1, :, :, 0, :] = sn
    m2[1, :, :, 1, :] = cs
    t['M2'] = m2.reshape(2, 64, 64 * 128).astype(np.float32)
    a = 2 * np.pi * np.arange(64)[:, None] * np.arange(64)[None, :] / 64.0
    bc = np.kron(np.eye(4), np.cos(a))
    bs = np.kron(np.eye(4), np.sin(a))
    t['Bcs'] = np.concatenate([bc, bs], axis=1).astype(np.float32)
    a = 2 * np.pi * np.arange(256)[:, None] * np.arange(256)[None, :] / 256.0
    t['D256'] = (4.0 * np.concatenate([np.cos(a), -np.sin(a)], axis=1)).astype(np.float32)
    rows = SEQ // 64
    row = np.repeat(np.arange(rows, dtype=np.float32), 64)
    col = np.tile(np.arange(64, dtype=np.float32), rows)
    inv = (10000.0 ** (-np.arange(0, 16, 2, dtype=np.float32) / 16.0)).astype(np.float32)
    ang_r = row[:, None] * inv[None, :]
    ang_c = col[:, None] * inv[None, :]
    angf = np.concatenate([ang_r, ang_r, ang_c, ang_c], axis=-1).astype(np.float32)
    rope = np.zeros((32, 2, TOK), np.float32)
    rope[:, 0, :CTXL] = 1.0
    rope[:, 0, CTXL:] = np.cos(angf).T
    rope[:, 1, CTXL:] = np.sin(angf).T
    t['rope'] = np.ascontiguousarray(np.tile(rope, (4, 1, 1)))
    pe_ = np.zeros((128, 2, 2, 16), np.float32)
    wins = (2, 4, 8, 16)
    for g, w in enumerate(wins):
        for si, n in enumerate((SEQ, CTXL)):
            tt = np.concatenate([np.arange(8), np.arange(n - 8, n)])
            lo = np.maximum(tt - w // 2, 0)
            hi = np.minimum(tt + w // 2 - 1, n - 1)
            cnt = (hi - lo + 1).astype(np.float32)
            pe_[(g % 2) * 64:(g % 2) * 64 + 64, g // 2, si, :] = (1.0 / cnt)[None, :]
    t['pedge'] = pe_.reshape(128, 64)
    return t


_CONSTS = None


def _get_consts():
    global _CONSTS
    if _CONSTS is None:
        _CONSTS = _const_tables()
    return _CONSTS


def build_program(debug=False, stop_after=None):
    nc = bass.Bass("TRN2", target_bir_lowering=False)
    P = Prog(nc)
    dbg = {}

    def din(name, shape, dt=F32):
        return nc.dram_tensor(name, list(shape), dt, kind="ExternalInput").ap()

    def dscr(name, shape, dt):
        return nc.dram_tensor(name, list(shape), dt, kind="Internal").ap()

    x_in = din("x", [2, SEQ, D])
    ctx_in = din("ctx", [2, CTXL, D])
    vecs_in = din("vecs", [128, NVEC])
    fng_in = din("fng", [1, D])
    ada_w = din("ada_w", [DEPTH, D, 6 * D])
    ada_b = din("ada_b", [DEPTH, 6 * D])
    w_in = din("w_in", [DEPTH, D, INC])
    fourier_w = din("fourier_w", [DEPTH, 256, 256])
    pool_w = din("pool_w", [DEPTH, 4, 64, 64])
    w_uq = din("w_uq", [DEPTH, 256, 384])
    w_ukv = din("w_ukv", [DEPTH, 128, 512])
    w_out = din("w_out", [DEPTH, D, D])
    mlp_w1 = din("mlp_w1", [DEPTH, D, 4 * D])
    mlp_w2 = din("mlp_w2", [DEPTH, 4 * D, D])
    c_ident = din("ident", [128, 128])
    c_W1 = din("W1", [64, 128])
    c_M2 = din("M2", [2, 64, 64 * 128])
    c_Bcs = din("Bcs", [256, 512])
    c_D256 = din("D256", [256, 512])
    c_rope = din("rope", [128, 2, TOK])
    c_pedge = din("pedge", [128, 64])

    y_out = nc.dram_tensor("y", [2, SEQ, D], F32, kind="ExternalOutput").ap()

    modsc = dscr("modsc", [DEPTH, 3, 6 * D], F32)
    xs1 = dscr("xs1", [2, TOK, D], F32)
    xs2 = dscr("xs2", [2, TOK, D], F32)
    gsc = dscr("gsc", [2, 8, 128, TOK], BF16)
    qsc = dscr("qsc", [2, 4, 96, TOK], BF16)
    ksc = dscr("ksc", [2, 4, 96, TOK], BF16)
    vsc = dscr("vsc", [2, TOK, 384], BF16)
    ufsc = dscr("ufsc", [2, 2, 128, TOK], BF16)

    if debug:
        def dbg_out(name, shape, dt=F32):
            dbg[name] = nc.dram_tensor("dbg_" + name, list(shape), dt, kind="ExternalOutput").ap()
            return dbg[name]
    A = Arena(nc, 16896, 227328)

    psall = nc.alloc_psum_tensor("psall", [128, 4096], F32)
    ps = [psall[:, i * 512:(i + 1) * 512] for i in range(8)]
    psb16 = [psall[:, i * 512:(i + 1) * 512].bitcast(BF16) for i in range(8)]
    psg = [psall[:, g * 1024:(g + 1) * 1024] for g in range(4)]

    ident = A.alloc("ident", [128, 128], BF16)
    ones = A.alloc("ones", [128, 128], BF16)
    vecs = A.alloc("vecs", [128, NVEC], F32)
    modT = [A.alloc(f"modT{l}", [128, 144], F32) for l in range(DEPTH)]
    scA = [A.alloc(f"scA{l}", [128, 24], F32) for l in range(DEPTH)]
    scB = [A.alloc(f"scB{l}", [128, 24], F32) for l in range(DEPTH)]
    pedge = A.alloc("pedge", [128, 64], F32)
    epsc = A.alloc("epsc", [128, 1], F32)

    def V(name, n=1, off=0):
        c = VEC_COLS[name] + off
        return vecs[:, c:c + n]

    P.add('pool', lambda e: e.dma_start(out=ident[:], in_=c_ident), w=['ident'], chan='wa')
    P.add('sp', lambda e: e.dma_start(out=vecs[:], in_=vecs_in), w=['vecs'], chan='ld0')
    P.add('sp', lambda e: e.dma_start(out=pedge[:], in_=c_pedge), w=['pedge'], chan='ld1')
    P.add('dve', lambda e: e.memset(ones[:], 1.0), w=['ones'])
    P.add('dve', lambda e: e.memset(epsc[:], EPS), w=['epsc'])

    A.push()
    adaw = A.alloc("adaw", [128, 8, 6 * D], BF16)
    adab = A.alloc("adab", [1, 6 * D], BF16)
    sT = A.alloc("sT", [128, 24], BF16)
    modrow = A.alloc("modrow", [3, 6 * D], F32)
    P.add('act', lambda e: e.activation(out=sT[:], in_=V('cT', 24), func=AF.Silu), r=['vecs'], w=['sT'])
    for l in range(DEPTH):
        for k in range(8):
            P.add('pool', lambda e, l=l, k=k: e.dma_start(out=adaw[:, k, :], in_=ada_w[l, k * 128:(k + 1) * 128, :]),
                  w=[('adaw', k)], chan='wa')
        P.add('pool', lambda e, l=l: e.dma_start(out=adab[:], in_=ada_b[l:l + 1, :]), w=['adab'], chan='wb')
        def colform(e, l=l):
            ins = None
            for m in range(48):
                for k in range(8):
                    e.matmul(ps[0][:, m * 3:m * 3 + 3], adaw[:, k, m * 128:(m + 1) * 128], sT[:, k * 3:k * 3 + 3],
                             start=(k == 0), stop=False)
                ins = e.matmul(ps[0][:, m * 3:m * 3 + 3], adab[0:1, m * 128:(m + 1) * 128], ones[0:1, 0:3],
                               start=False, stop=True)
            return ins
        P.add('pe', colform, r=[('adaw', k) for k in range(8)] + ['adab', 'sT', 'ones'], w=['ps0'])
        P.add('dve', lambda e, l=l: e.tensor_copy(modT[l][:], ps[0][:, 0:144]), r=['ps0'], w=[('modT', l)])
        for blk in range(12):
            bank = 1 + blk % 2
            def rowform(e, blk=blk, bank=bank):
                for k in range(8):
                    e.matmul(ps[bank][0:3, :], sT[:, k * 3:k * 3 + 3], adaw[:, k, blk * 512:(blk + 1) * 512],
                             start=(k == 0), stop=False)
                return e.matmul(ps[bank][0:3, :], ones[0:1, 0:3], adab[0:1, blk * 512:(blk + 1) * 512],
                                start=False, stop=True)
            P.add('pe', rowform, r=[('adaw', k) for k in range(8)] + ['adab', 'sT', 'ones'], w=[f'ps{bank}'])
            P.add('act', lambda e, blk=blk, bank=bank: e.activation(out=modrow[:, blk * 512:(blk + 1) * 512],
                                                                  in_=ps[bank][0:3, :], func=AF.Copy),
                  r=[f'ps{bank}'], w=['modrow'])
        P.add('sp', lambda e, l=l: e.dma_start(out=modsc[l], in_=modrow[:]), r=['modrow'], w=[('modsc', l)], chan='st0')
        for (dst, gname, c0) in ((scA[l], f'n1g{l}', 24), (scB[l], f'n2g{l}', 96)):
            P.add('dve', lambda e, dst=dst, c0=c0, l=l: e.tensor_scalar(dst[:], modT[l][:, c0:c0 + 24], 1.0, None, ALU.add),
                  r=[('modT', l)], w=[('sc', l, c0)])
            P.add('dve', lambda e, dst=dst, gname=gname: e.tensor_tensor(
                dst[:].rearrange("p (k m) -> p k m", m=3), dst[:].rearrange("p (k m) -> p k m", m=3),
                V(gname, 8).unsqueeze(2).to_broadcast([128, 8, 3]), ALU.mult),
                r=[('sc', l, c0), 'vecs'], w=[('sc', l, c0)])
    if debug:
        d_modT = dbg_out("modT", [DEPTH, 128, 144])
        d_scA = dbg_out("scA", [DEPTH, 128, 24])
        for l in range(DEPTH):
            P.add('sp', lambda e, l=l: e.dma_start(out=d_modT[l], in_=modT[l][:]), r=[('modT', l)], chan='st1')
            P.add('sp', lambda e, l=l: e.dma_start(out=d_scA[l], in_=scA[l][:]), r=[('sc', l, 24)], chan='st1')
    P.barrier()
    A.pop()


    cnt = {'pb': 0}

    def nb():
        b = 4 + cnt['pb'] % 4
        cnt['pb'] += 1
        return b

    def src_rows(l, s, t0, n):
        if l == 0:
            if t0 < CTXL:
                return ctx_in[s, t0:t0 + n, :]
            return x_in[s, t0 - CTXL:t0 - CTXL + n, :]
        return xs2[s, t0:t0 + n, :]

    def rsqrt_chain(dst, src_ps, inv_n, key_src, key_dst, shape_p=128):
        P.add('dve', lambda e: e.tensor_scalar(dst, src_ps, inv_n, EPS, ALU.mult, ALU.add), r=[key_src], w=[key_dst])
        P.add('act', lambda e: e.activation(out=dst, in_=dst, func=AF.Sqrt), r=[key_dst], w=[key_dst])
        P.add('dve', lambda e: e.reciprocal(dst, dst), r=[key_dst], w=[key_dst])

    tilectr = {'n': 0}

    def norm_stage(nb_, l, s, t0, ntok, sc_t, sh_t, m, src_fn, defer=False, split=False):
        tb = tilectr['n'] % len(nb_['hT'])
        tilectr['n'] += 1
        nsub = ntok // 128
        xt = nb_['xt']
        xn = nb_['xn']
        nxn = len(xn)
        hT = nb_['hT'][tb]
        ssq = nb_['ssq'][tb]

        def tr_op(j):
            xb = j % nxn
            def tr(e, j=j, xb=xb):
                for k in range(8):
                    bank, slot = k // 2, k % 2
                    e.transpose(psb16[bank][:, slot * 512 + j * 128: slot * 512 + (j + 1) * 128],
                                xn[xb][:, k * 128:(k + 1) * 128], ident[:])
            P.add('pe', tr, r=[('xn', xb), 'ident'], w=[('pst', j)] + [f'ps{b}' for b in range(4)] if j == 0 else [('pst', j)])

        xlist = isinstance(xt, list)

        def xtj(j):
            return xt[j % len(xt)][:, :] if xlist else xt[:, j, :]

        def xkey(j):
            return ('xt', j % len(xt)) if xlist else ('xt', j)

        def load(j):
            src = src_fn(l, s, t0 + j * 128, 128)
            P.add('sp', lambda e, j=j, src=src: e.dma_start(out=xtj(j), in_=src), w=[xkey(j)], chan=f'xl{j % 2}')

        def square(j):
            xb = j % nxn
            junk = nb_['junk'] if nb_.get('junk') is not None else xn[xb]
            jkey = 'junk' if nb_.get('junk') is not None else ('xn', xb)
            P.add('act', lambda e, j=j, junk=junk: e.activation(out=junk[:], in_=xtj(j), func=AF.Square,
                                                                accum_out=ssq[:, j:j + 1]),
                  r=[xkey(j)], w=[jkey, ('ssq', tb, j)])

        def xnorm(j):
            xb = j % nxn
            P.add('dve', lambda e, j=j, xb=xb: e.tensor_scalar(xn[xb][:], xtj(j), ssq[:, j:j + 1], None, ALU.mult),
                  r=[xkey(j), ('ssq', tb, j)], w=[('xn', xb)])

        def loads_fn():
            for j in range(nsub):
                load(j)

        def compute_fn():
            for j in range(nsub):
                square(j)
            keys = [('ssq', tb, j) for j in range(nsub)]
            P.add('dve', lambda e: e.tensor_scalar(ssq[:, 0:nsub], ssq[:, 0:nsub], 1.0 / D, EPS, ALU.mult, ALU.add), r=keys, w=keys)
            P.add('act', lambda e: e.activation(out=ssq[:, 0:nsub], in_=ssq[:, 0:nsub], func=AF.Sqrt), r=keys, w=keys)
            P.add('dve', lambda e: e.reciprocal(ssq[:, 0:nsub], ssq[:, 0:nsub]), r=keys, w=keys)
            for j in range(nsub):
                xnorm(j)

        def part_b():
            if defer or split is True:
                for j in range(nsub):
                    tr_op(j)
            for k in range(8):
                bank, slot = k // 2, k % 2
                src = psb16[bank][:, slot * 512: slot * 512 + ntok]
                scl = sc_t[:, k * 3 + m:k * 3 + m + 1]
                shf = sh_t[:, k * 3 + m:k * 3 + m + 1]
                rr = [('pst', j) for j in range(nsub)] + [('sc', l, 24), ('sc', l, 96), ('modT', l)]
                P.add('act', lambda e, k=k, src=src, scl=scl, shf=shf: e.activation(
                    out=hT[:, k, 0:ntok], in_=src, func=AF.Identity, scale=scl, bias=shf), r=rr, w=[('hT', tb, k)])

        def stream_fn():
            for j in range(nsub):
                load(j)
                square(j)
                rsqrt_chain(ssq[:, j:j + 1], ssq[:, j:j + 1], 1.0 / D, ('ssq', tb, j), ('ssq', tb, j))
                xnorm(j)
                tr_op(j)

        if split == 'stream':
            return tb, stream_fn, part_b
        if split:
            return tb, loads_fn, compute_fn, part_b
        for j in range(nsub):
            load(j)
            square(j)
            rsqrt_chain(ssq[:, j:j + 1], ssq[:, j:j + 1], 1.0 / D, ('ssq', tb, j), ('ssq', tb, j))
            xnorm(j)
            if not defer:
                tr_op(j)
        if defer:
            return tb, part_b
        part_b()
        return tb

    for l in range(DEPTH):
        last_layer = (l == DEPTH - 1)
        A.push()
        win = A.alloc("win", [128, 8, INC], BF16)
        wkr = A.alloc("wkr", [128, 8, 2, 32], BF16)
        wuqn = A.alloc("wuqn", [128, 2, 256], BF16)
        wuqp = A.alloc("wuqp", [128, 2, 128], BF16)
        wuqr = A.alloc("wuqr", [128, 2, 128], BF16)
        wukvn = A.alloc("wukvn", [128, 256], BF16)
        wukvv = A.alloc("wukvv", [128, 256], BF16)
        weff = A.alloc("weff", [128, 10, D], BF16)
        A.push()
        woutb = A.alloc("woutb", [128, 8, D], BF16)
        wfb = A.alloc("wfb", [128, 2, 256], BF16)
        bcs = A.alloc("bcs", [128, 2, 512], BF16)
        wuqf = A.alloc("wuqf", [128, 2, 384], F32)
        wukvf = A.alloc("wukvf", [128, 512], F32)
        poolP = A.alloc("poolP", [128, 2, 128], BF16)
        poolPT = A.alloc("poolPT", [128, 2, 128], BF16)
        wcT = A.alloc("wcT", [128, 2, 2, 256], BF16)

        for k in range(8):
            P.add('pool', lambda e, k=k: e.dma_start(out=win[:, k, :], in_=w_in[l, k * 128:(k + 1) * 128, :]),
                  w=[('win', k)], chan='wa' if k % 2 == 0 else 'wb')
        for k in range(8):
            P.add('pool', lambda e, k=k: e.dma_start(out=woutb[:, k, :], in_=w_out[l, k * 128:(k + 1) * 128, :]),
                  w=[('woutb', k)], chan='wa' if k % 2 == 0 else 'wb')
        P.add('pool', lambda e: e.dma_start(out=wfb[:], in_=fourier_w[l].rearrange("(c p) f -> p c f", p=128)),
              w=['wfb'], chan='wa')
        P.add('pool', lambda e: e.dma_start(out=bcs[:], in_=c_Bcs.rearrange("(c p) f -> p c f", p=128)),
              w=['bcs'], chan='wb')
        P.add('sp', lambda e: e.dma_start(out=wuqf[:], in_=w_uq[l].rearrange("(c p) f -> p c f", p=128)),
              w=['wuqf'], chan='ld0')
        P.add('sp', lambda e: e.dma_start(out=wukvf[:], in_=w_ukv[l]), w=['wukvf'], chan='ld1')
        P.add('dve', lambda e: e.memset(poolP[:], 0.0), w=['poolP'])
        for g in range(4):
            P.add('pool', lambda e, g=g: e.dma_start(out=poolP[(g % 2) * 64:(g % 2) * 64 + 64, g // 2, (g % 2) * 64:(g % 2) * 64 + 64],
                                                     in_=pool_w[l, g]), r=[], w=['poolP'], chan='wa')
        P.add('dve', lambda e: e.tensor_copy(wkr[:, :, 0, :], win[:, :, 1664:1696]), r=[('win', k) for k in range(8)], w=['wkr'])
        def krview(t, a, b):
            return t.rearrange("p k (h j) -> p k h j", h=2)[:, :, :, a:b]
        P.add('dve', lambda e: e.tensor_scalar(krview(wkr[:, :, 1, :], 0, 8), krview(win[:, :, 1664:1696], 8, 16),
                                               -1.0, None, ALU.mult), r=[('win', k) for k in range(8)], w=['wkr'])
        P.add('dve', lambda e: e.tensor_copy(krview(wkr[:, :, 1, :], 8, 16), krview(win[:, :, 1664:1696], 0, 8)),
              r=[('win', k) for k in range(8)], w=['wkr'])
        for kc in range(2):
            src4 = wuqf[:, kc, :].rearrange("p (h c) -> p h c", h=4)
            P.add('dve', lambda e, kc=kc, src4=src4: e.tensor_scalar(wuqn[:, kc, :].rearrange("p (h c) -> p h c", h=4), src4[:, :, 0:64],
                                                                    V(f'qng{l}', 1, kc), None, ALU.mult), r=['wuqf', 'vecs'], w=['wuqn'])
            P.add('dve', lambda e, kc=kc, src4=src4: e.tensor_scalar(wuqp[:, kc, :].rearrange("p (h c) -> p h c", h=4), src4[:, :, 64:96],
                                                                    V(f'qng{l}', 1, kc), None, ALU.mult), r=['wuqf', 'vecs'], w=['wuqp'])
            pv_ = lambda t, a, b, kc=kc: t[:, kc, :].rearrange("p (h f j) -> p h f j", h=4, f=2)[:, :, :, a:b]
            P.add('dve', lambda e, pv_=pv_: e.tensor_scalar(pv_(wuqr, 0, 8), pv_(wuqp, 8, 16), -1.0, None, ALU.mult), r=['wuqp'], w=['wuqr'])
            P.add('dve', lambda e, pv_=pv_: e.tensor_copy(pv_(wuqr, 8, 16), pv_(wuqp, 0, 8)), r=['wuqp'], w=['wuqr'])
        kv4 = wukvf[:, :].rearrange("p (h c) -> p h c", h=4)
        P.add('dve', lambda e: e.tensor_scalar(wukvn[:, :].rearrange("p (h c) -> p h c", h=4), kv4[:, :, 0:64], V(f'kvng{l}', 1), None, ALU.mult),
              r=['wukvf', 'vecs'], w=['wukvn'])
        P.add('dve', lambda e: e.tensor_scalar(wukvv[:, :].rearrange("p (h c) -> p h c", h=4), kv4[:, :, 64:128], V(f'kvng{l}', 1), None, ALU.mult),
              r=['wukvf', 'vecs'], w=['wukvv'])
        for (dst, srck) in ((4, 2), (5, 3), (8, 6), (9, 7)):
            P.add('act', lambda e, dst=dst, srck=srck: e.activation(out=weff[:, dst, :], in_=woutb[:, srck, :], func=AF.Copy),
                  r=[('woutb', srck)], w=[('weff', dst)])
        for cp in range(2):
            b = nb()
            def f1(e, cp=cp, b=b):
                ins = None
                for jc in range(2):
                    ins = e.matmul(ps[b][:, :], wfb[:, jc, cp * 128:(cp + 1) * 128], bcs[:, jc, :], start=(jc == 0), stop=(jc == 1))
                return ins
            P.add('pe', f1, r=['wfb', 'bcs'], w=[f'ps{b}'])
            P.add('act', lambda e, cp=cp, b=b: e.activation(out=wcT[:, :, cp, :], in_=ps[b][:, :].rearrange("p (t c) -> p t c", t=2),
                                                          func=AF.Copy, scale=1.0 / 512.0), r=[f'ps{b}'], w=['wcT'])
        for t in range(2):
            for cc in range(2):
                for half in range(2):
                    b = nb()
                    def f2(e, t=t, cc=cc, half=half, b=b):
                        ins = None
                        for cp in range(2):
                            ins = e.matmul(ps[b][:, :], wcT[:, t, cp, cc * 128:(cc + 1) * 128], woutb[:, cp, half * 512:(half + 1) * 512],
                                           start=(cp == 0), stop=(cp == 1))
                        return ins
                    P.add('pe', f2, r=['wcT', ('woutb', 0), ('woutb', 1)], w=[f'ps{b}'])
                    P.add('dve', lambda e, t=t, cc=cc, half=half, b=b: e.tensor_copy(weff[:, 2 * t + cc, half * 512:(half + 1) * 512], ps[b][:, :]),
                          r=[f'ps{b}'], w=[('weff', 2 * t + cc)])
        for cc in range(2):
            b = nb()
            P.add('pe', lambda e, cc=cc, b=b: e.transpose(psb16[b][:, 0:128], poolP[:, cc, :], ident[:]), r=['poolP', 'ident'], w=[f'ps{b}'])
            P.add('act', lambda e, cc=cc, b=b: e.activation(out=poolPT[:, cc, :], in_=psb16[b][:, 0:128], func=AF.Copy,
                                                          scale=V(f'pscale{l}', 1, cc)), r=[f'ps{b}', 'vecs'], w=['poolPT'])
            for half in range(2):
                b2 = nb()
                P.add('pe', lambda e, cc=cc, half=half, b2=b2: e.matmul(ps[b2][:, :], poolPT[:, cc, :], woutb[:, 4 + cc, half * 512:(half + 1) * 512],
                                                                       start=True, stop=True), r=['poolPT', ('woutb', 4 + cc)], w=[f'ps{b2}'])
                P.add('dve', lambda e, cc=cc, half=half, b2=b2: e.tensor_copy(weff[:, 6 + cc, half * 512:(half + 1) * 512], ps[b2][:, :]),
                      r=[f'ps{b2}'], w=[('weff', 6 + cc)])
        if debug and l == 0:
            d_weff = dbg_out("weff", [128, 10, D], BF16)
            P.add('sp', lambda e: e.dma_start(out=d_weff, in_=weff[:]), r=[('weff', c) for c in range(10)], chan='st1')
        P.barrier()
        A.pop()
        if stop_after == ('W', l):
            break

        A.push()
        nbuf = dict(
            xt=A.alloc("xt", [128, 4, D], F32),
            xn=[A.alloc("xn", [128, D], BF16) for _ in range(4)],
            hT=[A.alloc("hT", [128, 8, 512], BF16) for _ in range(2)],
            ssq=[A.alloc("ssq", [128, 4], F32) for _ in range(2)],
            junk=None,
        )
        uFt = [A.alloc("uFt", [128, 2, 512], BF16) for _ in range(2)]
        BW = 536
        bgb = [A.alloc("bgb", [128, 2, BW], BF16) for _ in range(2)]
        zb = [A.alloc("zb", [128, 2, BW], BF16) for _ in range(2)]
        ub = [A.alloc("ub", [128, 2, BW], BF16) for _ in range(2)]
        cgt = A.alloc("cgt", [128, 2, 512], BF16)
        ytmp = A.alloc("ytmp", [128, 520], F32)
        a2 = A.alloc("a2", [128, 2, BW], F32)
        a4 = A.alloc("a4", [128, 2, BW], F32)
        a8 = A.alloc("a8", [128, BW], F32)
        a16 = A.alloc("a16", [128, BW], F32)
        etmp = A.alloc("etmp", [128, 8], F32)
        sout = A.alloc("sout", [128, 2, 520], BF16)
        pout = A.alloc("pout", [128, 2, 520], BF16)
        cqT = A.alloc("cqT", [128, 3, 512], BF16)
        sqT = A.alloc("sqT", [128, 3, 512], BF16)
        cqn = A.alloc("cqn", [128, 3, 512], BF16)
        rqbc = A.alloc("rqbc", [128, 512], F32)
        rkvbc = A.alloc("rkvbc", [128, 512], F32)
        ropet = [A.alloc("ropet", [128, 2, 512], F32) for _ in range(2)]
        rt1 = A.alloc("rt1", [128, 512], F32)
        rt2 = A.alloc("rt2", [128, 512], F32)
        kt1 = A.alloc("kt1", [32, 512], F32)
        kt2 = A.alloc("kt2", [32, 512], F32)
        qn = A.alloc("qn", [128, 2, 512], BF16)
        qpe = A.alloc("qpe", [128, 512], BF16)
        kn = A.alloc("kn", [128, 2, 512], BF16)
        kpe = A.alloc("kpe", [32, 512], BF16)
        vo = A.alloc("vo", [128, 4, 384], BF16)
        P.add('dve', lambda e: e.memset(vo[:], 1.0), w=['vo'])
        WIN_ALL = [('win', k) for k in range(8)]
        HT = nbuf['hT']
        stc = {'n': 0}

        def stchan():
            stc['n'] += 1
            return f"sa{stc['n'] % 6}"

        def proj_fm(tb, ncols, ntok, b, lhs):
            def f(e):
                for k in range(8):
                    e.matmul(ps[b][0:ncols, 0:ntok], lhs(k), HT[tb][:, k, 0:ntok], start=(k == 0), stop=(k == 7))
            P.add('pe', f, r=WIN_ALL + ['wkr'] + [('hT', tb, k) for k in range(8)], w=[f'ps{b}'])

        def wcols(c0, n=128):
            return lambda k: win[:, k, c0:c0 + n]

        def tile_info(s, ti):
            is_ctx = (ti == 0)
            ntok = CTXL if is_ctx else 512
            t0 = 0 if is_ctx else CTXL + (ti - 1) * 512
            return is_ctx, ntok, t0

        def stageN(s, ti):
            is_ctx, ntok, t0 = tile_info(s, ti)
            m = 2 if is_ctx else s
            tb, loads_fn, compute_fn, part_b = norm_stage(nbuf, l, s, t0, ntok, scA[l], modT[l], m, src_rows, split=True)
            def rope_fn():
                P.add('sp', lambda e: e.dma_start(out=ropet[tb][:, :, 0:ntok], in_=c_rope[:, :, t0:t0 + ntok]), w=[('ropet', tb)], chan=f'rp{tb}')
            return dict(tb=tb, loads=loads_fn, compute=compute_fn, part_b=part_b, rope=rope_fn)

        def act_chain(dst, src_ps, inv_n, key_src, key_dst):
            P.add('act', lambda e: e.activation(out=dst, in_=src_ps, func=AF.Sqrt, scale=inv_n, bias=epsc[:, 0:1]), r=[key_src, 'epsc'], w=[key_dst])
            P.add('dve', lambda e: e.reciprocal(dst, dst), r=[key_dst], w=[key_dst])

        def stageP(s, ti, tb, mid_hook=None):
            is_ctx, ntok, t0 = tile_info(s, ti)
            nsub = ntok // 128
            first = is_ctx or ti == 1
            lastt = is_ctx or ti == 8
            full = not (is_ctx and last_layer)
            pb, pp = tb, 1 - tb
            for c3 in range(3):
                if c3 < 2 and not full:
                    continue
                b = nb()
                proj_fm(tb, 128, ntok, b, wcols(1280 + c3 * 128))
                P.add('act', lambda e, b=b, c3=c3: e.activation(out=cqT[:, c3, 0:ntok], in_=ps[b][:, 0:ntok], func=AF.Copy),
                      r=[f'ps{b}'], w=[('cqT', c3)])
                P.add('act', lambda e, b=b, c3=c3: e.activation(out=sqT[:, c3, 0:ntok], in_=ps[b][:, 0:ntok], func=AF.Square),
                      r=[f'ps{b}'], w=[('sqT', c3)])
            bk = nb()
            proj_fm(tb, 32, ntok, bk, lambda k: wkr[:, k, 0, :])
            P.add('act', lambda e: e.activation(out=kt1[:, 0:ntok], in_=ps[bk][0:32, 0:ntok], func=AF.Copy), r=[f'ps{bk}'], w=['kt1'])
            bkr = nb()
            proj_fm(tb, 32, ntok, bkr, lambda k: wkr[:, k, 1, :])
            P.add('act', lambda e: e.activation(out=kt2[:, 0:ntok], in_=ps[bkr][0:32, 0:ntok], func=AF.Copy), r=[f'ps{bkr}'], w=['kt2'])
            P.add('dve', lambda e: e.tensor_tensor(kt1[:, 0:ntok], kt1[:, 0:ntok], ropet[tb][0:32, 0, 0:ntok], ALU.mult), r=['kt1', ('ropet', tb)], w=['kt1'])
            P.add('dve', lambda e: e.tensor_tensor(kt2[:, 0:ntok], kt2[:, 0:ntok], ropet[tb][0:32, 1, 0:ntok], ALU.mult), r=['kt2', ('ropet', tb)], w=['kt2'])
            P.add('dve', lambda e: e.tensor_tensor(kpe[:, 0:ntok], kt1[:, 0:ntok], kt2[:, 0:ntok], ALU.add), r=['kt1', 'kt2'], w=['kpe'])
            for h in range(4):
                P.add('sp', lambda e, h=h: e.dma_start(out=ksc[s, h, 64:96, t0:t0 + ntok], in_=kpe[:, 0:ntok]), r=['kpe'], chan=stchan())
            if full:
                b = nb()
                def fq(e, b=b):
                    e.matmul(ps[b][:, 0:ntok], ones[:, :], sqT[:, 0, 0:ntok], start=True, stop=False)
                    e.matmul(ps[b][:, 0:ntok], ones[:, :], sqT[:, 1, 0:ntok], start=False, stop=True)
                P.add('pe', fq, r=[('sqT', 0), ('sqT', 1), 'ones'], w=[f'ps{b}'])
                act_chain(rqbc[:, 0:ntok], ps[b][:, 0:ntok], 1.0 / 256.0, f'ps{b}', 'rqbc')
                P.add('dve', lambda e: e.tensor_tensor(cqn[:, 0:2, 0:ntok], cqT[:, 0:2, 0:ntok], rqbc[:, 0:ntok].unsqueeze(1).to_broadcast([128, 2, ntok]), ALU.mult),
                      r=[('cqT', 0), ('cqT', 1), 'rqbc'], w=[('cqn', 0), ('cqn', 1)])
            b = nb()
            P.add('pe', lambda e, b=b: e.matmul(ps[b][:, 0:ntok], ones[:, :], sqT[:, 2, 0:ntok], start=True, stop=True),
                  r=[('sqT', 2), 'ones'], w=[f'ps{b}'])
            act_chain(rkvbc[:, 0:ntok], ps[b][:, 0:ntok], 1.0 / 128.0, f'ps{b}', 'rkvbc')
            P.add('dve', lambda e: e.tensor_tensor(cqn[:, 2, 0:ntok], cqT[:, 2, 0:ntok], rkvbc[:, 0:ntok], ALU.mult),
                  r=[('cqT', 2), 'rkvbc'], w=[('cqn', 2)])
            if full:
                for cc in range(2):
                    b = nb()
                    proj_fm(tb, 128, ntok, b, wcols(cc * 128))
                    P.add('act', lambda e, b=b, cc=cc: e.activation(out=uFt[pb][:, cc, 0:ntok], in_=ps[b][:, 0:ntok], func=AF.Copy),
                          r=[f'ps{b}'], w=[('uFt', pb, cc)])
                P.add('sp', lambda e: e.dma_start(out=ufsc[s, :, :, t0:t0 + ntok].rearrange("c p t -> p c t"), in_=uFt[pb][:, :, 0:ntok]),
                      r=[('uFt', pb, 0), ('uFt', pb, 1)], chan=stchan())
                for (buf, key) in ((bgb, 'bgb'), (zb, 'zb'), (ub, 'ub')):
                    if first:
                        P.add('pool', lambda e, buf=buf: e.memset(buf[pb][:, :, 0:16], 0.0), w=[(key, pb, 'c')])
                    else:
                        P.add('pool', lambda e, buf=buf: e.tensor_copy(buf[pb][:, :, 0:16], buf[pp][:, :, 512:528]),
                              r=[(key, pp, 0), (key, pp, 1)], w=[(key, pb, 'c')])
                    if lastt:
                        P.add('pool', lambda e, buf=buf: e.memset(buf[pb][:, :, 16 + ntok:24 + ntok], 0.0), w=[(key, pb, 't')])
                for cc in range(2):
                    b = nb()
                    proj_fm(tb, 128, ntok, b, wcols(256 + cc * 128))
                    P.add('act', lambda e, b=b, cc=cc: e.activation(out=bgb[pb][:, cc, 16:16 + ntok], in_=ps[b][:, 0:ntok], func=AF.Copy),
                          r=[f'ps{b}'], w=[('bgb', pb, cc)])
                    b1 = nb()
                    proj_fm(tb, 128, ntok, b1, wcols(512 + cc * 128))
                    P.add('act', lambda e, b1=b1, cc=cc: e.activation(out=cgt[:, cc, 0:ntok], in_=ps[b1][:, 0:ntok], func=AF.Copy),
                          r=[f'ps{b1}'], w=[('cgt', cc)])
                    b2 = nb()
                    proj_fm(tb, 128, ntok, b2, wcols(768 + cc * 128))
                    P.add('act', lambda e, b2=b2, cc=cc: e.activation(out=zb[pb][:, cc, 16:16 + ntok], in_=ps[b2][:, 0:ntok], func=AF.Copy),
                          r=[f'ps{b2}'], w=[('zb', pb, cc)])
                    P.add('dve', lambda e, cc=cc: e.tensor_tensor(zb[pb][:, cc, 16:16 + ntok], zb[pb][:, cc, 16:16 + ntok], cgt[:, cc, 0:ntok], ALU.mult),
                          r=[('zb', pb, cc), ('cgt', cc)], w=[('zb', pb, cc)])
                for cc in range(2):
                    b = nb()
                    proj_fm(tb, 128, ntok, b, wcols(1024 + cc * 128))
                    P.add('act', lambda e, b=b, cc=cc: e.activation(out=ub[pb][:, cc, 16:16 + ntok], in_=ps[b][:, 0:ntok], func=AF.Copy),
                          r=[f'ps{b}'], w=[('ub', pb, cc)])
            if mid_hook is not None:
                mid_hook()
            for cp in range(2):
                b = nb()
                P.add('pe', lambda e, b=b, cp=cp: e.matmul(ps[b][:, 0:ntok], wukvn[:, cp * 128:(cp + 1) * 128], cqn[:, 2, 0:ntok], start=True, stop=True),
                      r=['wukvn', ('cqn', 2)], w=[f'ps{b}'])
                P.add('act', lambda e, b=b, cp=cp: e.activation(out=kn[:, cp, 0:ntok], in_=ps[b][:, 0:ntok], func=AF.Copy), r=[f'ps{b}'], w=[('kn', cp)])
                for hh in range(2):
                    P.add('sp', lambda e, cp=cp, hh=hh: e.dma_start(out=ksc[s, 2 * cp + hh, 0:64, t0:t0 + ntok], in_=kn[hh * 64:(hh + 1) * 64, cp, 0:ntok]),
                          r=[('kn', cp)], chan=stchan())
            for j in range(nsub):
                b = nb()
                P.add('pe', lambda e, b=b, j=j: e.matmul(ps[b][:, 0:256], cqn[:, 2, j * 128:(j + 1) * 128], wukvv[:, :], start=True, stop=True),
                      r=['wukvv', ('cqn', 2)], w=[f'ps{b}'])
                P.add('act', lambda e, b=b, j=j: e.activation(
                    out=vo[:, j, :].rearrange("p (g b d) -> p g b d", g=2, b=3)[:, :, 0:3:2, :],
                    in_=ps[b][:, 0:256].rearrange("p (g i d) -> p g i d", g=2, i=2), func=AF.Copy), r=[f'ps{b}'], w=['vo'])
            P.add('sp', lambda e: e.dma_start(out=vsc[s, t0:t0 + ntok, :].rearrange("(j p) c -> p j c", p=128), in_=vo[:, 0:nsub, :]),
                  r=['vo'], chan=stchan())
            if full:
                for cp in range(2):
                    b = nb()
                    def fqn(e, b=b, cp=cp):
                        for kc in range(2):
                            e.matmul(ps[b][:, 0:ntok], wuqn[:, kc, cp * 128:(cp + 1) * 128], cqn[:, kc, 0:ntok], start=(kc == 0), stop=(kc == 1))
                    P.add('pe', fqn, r=['wuqn', ('cqn', 0), ('cqn', 1)], w=[f'ps{b}'])
                    P.add('act', lambda e, b=b, cp=cp: e.activation(out=qn[:, cp, 0:ntok], in_=ps[b][:, 0:ntok], func=AF.Copy), r=[f'ps{b}'], w=[('qn', cp)])
                    for hh in range(2):
                        P.add('sp', lambda e, cp=cp, hh=hh: e.dma_start(out=qsc[s, 2 * cp + hh, 0:64, t0:t0 + ntok], in_=qn[hh * 64:(hh + 1) * 64, cp, 0:ntok]),
                              r=[('qn', cp)], chan=stchan())
                bq = nb()
                def fqp(e, bq=bq):
                    for kc in range(2):
                        e.matmul(ps[bq][:, 0:ntok], wuqp[:, kc, :], cqn[:, kc, 0:ntok], start=(kc == 0), stop=(kc == 1))
                P.add('pe', fqp, r=['wuqp', ('cqn', 0), ('cqn', 1)], w=[f'ps{bq}'])
                P.add('act', lambda e: e.activation(out=rt1[:, 0:ntok], in_=ps[bq][:, 0:ntok], func=AF.Copy), r=[f'ps{bq}'], w=['rt1'])
                br = nb()
                def fqr(e, br=br):
                    for kc in range(2):
                        e.matmul(ps[br][:, 0:ntok], wuqr[:, kc, :], cqn[:, kc, 0:ntok], start=(kc == 0), stop=(kc == 1))
                P.add('pe', fqr, r=['wuqr', ('cqn', 0), ('cqn', 1)], w=[f'ps{br}'])
                P.add('act', lambda e: e.activation(out=rt2[:, 0:ntok], in_=ps[br][:, 0:ntok], func=AF.Copy), r=[f'ps{br}'], w=['rt2'])
                P.add('dve', lambda e: e.tensor_tensor(rt1[:, 0:ntok], rt1[:, 0:ntok], ropet[tb][:, 0, 0:ntok], ALU.mult), r=['rt1', ('ropet', tb)], w=['rt1'])
                P.add('dve', lambda e: e.tensor_tensor(rt2[:, 0:ntok], rt2[:, 0:ntok], ropet[tb][:, 1, 0:ntok], ALU.mult), r=['rt2', ('ropet', tb)], w=['rt2'])
                P.add('dve', lambda e: e.tensor_tensor(qpe[:, 0:ntok], rt1[:, 0:ntok], rt2[:, 0:ntok], ALU.add), r=['rt1', 'rt2'], w=['qpe'])
                for h in range(4):
                    P.add('sp', lambda e, h=h: e.dma_start(out=qsc[s, h, 64:96, t0:t0 + ntok], in_=qpe[h * 32:(h + 1) * 32, 0:ntok]), r=['qpe'], chan=stchan())
            def convpool():
                lo = 16 if first else 8
                hi = 16 + ntok if lastt else 8 + ntok
                n = hi - lo
                W_ = 24 + ntok if lastt else 16 + ntok
                zkeys = [('zb', pb, 0), ('zb', pb, 1), ('zb', pb, 'c'), ('zb', pb, 't')]
                bkeys = [('bgb', pb, 0), ('bgb', pb, 1), ('bgb', pb, 'c'), ('bgb', pb, 't')]
                ukeys = [('ub', pb, 0), ('ub', pb, 1), ('ub', pb, 'c'), ('ub', pb, 't')]
                for cc in range(2):
                    cw = lambda tap, cc=cc: V(f'convw{l}', 1, tap * 2 + cc)
                    P.add('dve', lambda e, cc=cc, cw=cw: e.tensor_scalar(ytmp[:, 0:n], zb[pb][:, cc, lo:hi], cw(1), None, ALU.mult),
                          r=zkeys + ['vecs'], w=['ytmp'])
                    for (tap, sh) in ((0, -1), (2, 1)):
                        P.add('dve', lambda e, cc=cc, cw=cw, tap=tap, sh=sh: e.scalar_tensor_tensor(ytmp[:, 0:n], zb[pb][:, cc, lo + sh:hi + sh], cw(tap), ytmp[:, 0:n], ALU.mult, ALU.add),
                              r=zkeys + ['vecs', 'ytmp'], w=['ytmp'])
                    P.add('dve', lambda e, cc=cc: e.tensor_tensor(sout[:, cc, 0:n], bgb[pb][:, cc, lo:hi], ytmp[:, 0:n], ALU.mult),
                          r=bkeys + ['ytmp'], w=['sout'])
                P.add('dve', lambda e: e.tensor_tensor(a2[:, :, 1:W_], ub[pb][:, :, 0:W_ - 1], ub[pb][:, :, 1:W_], ALU.add), r=ukeys, w=['a2'])
                P.add('dve', lambda e: e.tensor_tensor(a4[:, :, 2:W_ - 1], a2[:, :, 1:W_ - 2], a2[:, :, 3:W_], ALU.add), r=['a2'], w=['a4'])
                P.add('dve', lambda e: e.tensor_tensor(a8[:, 4:W_ - 3], a4[:, 1, 2:W_ - 5], a4[:, 1, 6:W_ - 1], ALU.add), r=['a4'], w=['a8'])
                P.add('dve', lambda e: e.tensor_tensor(a16[64:128, 8:W_ - 7], a8[64:128, 4:W_ - 11], a8[64:128, 12:W_ - 3], ALU.add), r=['a8'], w=['a16'])
                sels = [(0, 0, 64, a2[0:64, 0, :]), (0, 64, 128, a4[64:128, 0, :]), (1, 0, 64, a8[0:64, :]), (1, 64, 128, a16[64:128, :])]
                si = 1 if is_ctx else 0
                for (cc, p0, p1, sel) in sels:
                    P.add('dve', lambda e, cc=cc, p0=p0, p1=p1, sel=sel: e.scalar_tensor_tensor(
                        pout[p0:p1, cc, 0:n], sel[:, lo:hi], V('invw', 1, cc)[p0:p1, :], ub[pb][p0:p1, cc, lo:hi], ALU.mult, ALU.subtract),
                        r=ukeys + ['a2', 'a4', 'a8', 'a16', 'vecs'], w=['pout'])
                    pe0 = cc * 32 + si * 16
                    if first:
                        P.add('pool', lambda e, p0=p0, p1=p1, sel=sel, pe0=pe0: e.tensor_tensor(
                            etmp[p0:p1, :], sel[:, 16:24], pedge[p0:p1, pe0:pe0 + 8], ALU.mult), r=['a2', 'a4', 'a8', 'a16', 'pedge'], w=['etmp'])
                        P.add('pool', lambda e, cc=cc, p0=p0, p1=p1: e.tensor_tensor(
                            pout[p0:p1, cc, 16 - lo:24 - lo], etmp[p0:p1, :], ub[pb][p0:p1, cc, 16:24], ALU.subtract),
                            r=ukeys + ['etmp', 'pout'], w=['pout'])
                    if lastt:
                        P.add('pool', lambda e, p0=p0, p1=p1, sel=sel, pe0=pe0: e.tensor_tensor(
                            etmp[p0:p1, :], sel[:, 8 + ntok:16 + ntok], pedge[p0:p1, pe0 + 8:pe0 + 16], ALU.mult), r=['a2', 'a4', 'a8', 'a16', 'pedge'], w=['etmp'])
                        P.add('pool', lambda e, cc=cc, p0=p0, p1=p1: e.tensor_tensor(
                            pout[p0:p1, cc, 8 + ntok - lo:16 + ntok - lo], etmp[p0:p1, :], ub[pb][p0:p1, cc, 8 + ntok:16 + ntok], ALU.subtract),
                            r=ukeys + ['etmp', 'pout'], w=['pout'])
                tok_lo = t0 + (lo - 16)
                P.add('sp', lambda e: e.dma_start(out=gsc[s, 4:6, :, tok_lo:tok_lo + n].rearrange("c p t -> p c t"), in_=sout[:, :, 0:n]),
                      r=['sout'], chan=stchan())
                P.add('sp', lambda e: e.dma_start(out=gsc[s, 6:8, :, tok_lo:tok_lo + n].rearrange("c p t -> p c t"), in_=pout[:, :, 0:n]),
                      r=['pout'], chan=stchan())
            return convpool if full else None

        tiles = [(s, ti) for s in range(int(os.environ.get('DBG_NS', '2'))) for ti in range(int(os.environ.get('DBG_NT', '9')))]
        prev_cp = None
        NS = {}
        def getN(i_):
            if i_ < len(tiles) and i_ not in NS:
                NS[i_] = stageN(*tiles[i_])
            return NS.get(i_)
        n0 = getN(0)
        n0['loads'](); n0['rope'](); n0['compute'](); n0['part_b']()
        n1 = getN(1)
        if n1 is not None:
            n1['loads'](); n1['compute']()
        for i_, tl in enumerate(tiles):
            n_next = getN(i_ + 1)
            n_next2 = getN(i_ + 2)
            if n_next2 is not None:
                n_next2['loads']()
            def hook(n_next=n_next, n_next2=n_next2):
                if n_next is not None:
                    n_next['part_b']()
                    n_next['rope']()
                if n_next2 is not None:
                    n_next2['compute']()
            cpf = stageP(tl[0], tl[1], NS[i_]['tb'], hook)
            if prev_cp is not None:
                prev_cp()
            prev_cp = cpf
        if prev_cp is not None:
            prev_cp()
        if debug and l == 0 and not os.environ.get('DBG_NODUMP'):
            P.barrier()
            d_uF = dbg_out("uF", [2, 128, TOK], BF16)
            for c_ in range(2):
                P.add('sp', lambda e, c_=c_: e.dma_start(out=d_uF[c_], in_=ufsc[0, c_]), chan='st1')
            for nm, src_, nch in (("gsc", gsc, 8), ("qsc", qsc, 4), ("ksc", ksc, 4)):
                dd = dbg_out(nm, list(src_.shape[1:]), BF16)
                for c_ in range(nch):
                    P.add('sp', lambda e, dd=dd, src_=src_, c_=c_: e.dma_start(out=dd[c_], in_=src_[0, c_]), chan='st1')
            dd = dbg_out("vsc", [TOK, 384], BF16)
            for c_ in range(4):
                P.add('sp', lambda e, dd=dd, c_=c_: e.dma_start(out=dd[c_ * 1088:(c_ + 1) * 1088, :], in_=vsc[0, c_ * 1088:(c_ + 1) * 1088, :]), chan='st1')
        P.barrier()
        A.pop()
        if stop_after == ('A', l):
            break

        A.push()
        uF = A.alloc("uF", [128, 2, SEQ], BF16)
        u2 = A.alloc("u2", [64, 64, 256], BF16)
        Acc = A.alloc("Acc", [64, 128, 128], BF16)
        M2b = A.alloc("M2b", [64, 2, 8192], BF16)
        W1b = A.alloc("W1b", [64, 128], BF16)
        Xo = A.alloc("Xo", [128, 2, SEQ], BF16)
        uc = A.alloc("uc", [128, 2, CTXL], BF16)
        uct = A.alloc("uct", [128, 2, 256], BF16)
        d256 = A.alloc("d256", [128, 2, 512], BF16)
        xc = A.alloc("xc", [128, 2, 2, CTXL], BF16)
        P.add('pool', lambda e: e.dma_start(out=W1b[:], in_=c_W1), w=['W1b'], chan='wa')
        for c_ in range(2):
            P.add('pool', lambda e, c_=c_: e.dma_start(out=M2b[:, c_, :], in_=c_M2[c_]), w=[('M2b', c_)], chan='wb' if c_ else 'wa')
        P.add('pool', lambda e: e.dma_start(out=d256[:], in_=c_D256.rearrange("(c p) f -> p c f", p=128)), w=['d256'], chan='wa')
        fb = {'n': 0}

        def fbank():
            b = fb['n'] % 8
            fb['n'] += 1
            return b
        for s in range(int(os.environ.get('DBG_NS', '2'))):
            P.add('sp', lambda e: e.dma_start(out=uF[:], in_=ufsc[s, :, :, CTXL:TOK].rearrange("c p t -> p c t")), r=[('ufsc', s)], w=['uF'], chan='ld0')
            for g4 in range(16):
                b = fbank()
                def t1(e, g4=g4, b=b):
                    for i4 in range(4):
                        n2 = g4 * 4 + i4
                        for cc in range(2):
                            e.transpose(psb16[b][0:64, i4 * 256 + cc * 128: i4 * 256 + (cc + 1) * 128],
                                        uF[:, cc, :].rearrange("p (a n) -> p n a", n=64)[:, n2, :], ident[:])
                P.add('pe', t1, r=['uF', 'ident'], w=[f'ps{b}'])
                if g4 % 2 == 0:
                    P.add('act', lambda e, g4=g4, b=b: e.activation(out=u2[:, g4 * 4:(g4 + 1) * 4, :], in_=psb16[b][0:64, :].rearrange("p (i c) -> p i c", i=4), func=AF.Copy),
                          r=[f'ps{b}'], w=['u2'])
                else:
                    P.add('dve', lambda e, g4=g4, b=b: e.tensor_copy(u2[:, g4 * 4:(g4 + 1) * 4, :], psb16[b][0:64, :].rearrange("p (i c) -> p i c", i=4)),
                          r=[f'ps{b}'], w=['u2'], free_psum=True)
            for cc in range(2):
                for g4 in range(32):
                    b = fbank()
                    def s1(e, g4=g4, b=b, cc=cc):
                        for i4 in range(4):
                            ch = cc * 128 + g4 * 4 + i4
                            e.matmul(ps[b][0:64, i4 * 128:(i4 + 1) * 128], u2[:, :, ch], W1b[:, :], start=True, stop=True)
                    P.add('pe', s1, r=['u2', 'W1b'], w=[f'ps{b}'])
                    if g4 % 2 == 0:
                        P.add('act', lambda e, g4=g4, b=b: e.activation(out=Acc[:, g4 * 4:(g4 + 1) * 4, :], in_=ps[b][0:64, :].rearrange("p (i c) -> p i c", i=4), func=AF.Copy),
                              r=[f'ps{b}'], w=['Acc'])
                    else:
                        P.add('dve', lambda e, g4=g4, b=b: e.tensor_copy(Acc[:, g4 * 4:(g4 + 1) * 4, :], ps[b][0:64, :].rearrange("p (i c) -> p i c", i=4)),
                              r=[f'ps{b}'], w=['Acc'], free_psum=True)
                for g4 in range(16):
                    b = fbank()
                    def s2(e, g4=g4, b=b):
                        for i4 in range(4):
                            k1 = g4 * 4 + i4
                            for c_ in range(2):
                                e.matmul(ps[b][:, i4 * 128:(i4 + 1) * 128], Acc[:, :, c_ * 64 + k1], M2b[:, c_, k1 * 128:(k1 + 1) * 128],
                                         start=(c_ == 0), stop=(c_ == 1))
                    P.add('pe', s2, r=['Acc', ('M2b', 0), ('M2b', 1)], w=[f'ps{b}'])
                    if g4 % 2 == 0:
                        P.add('act', lambda e, g4=g4, b=b: e.activation(
                            out=Xo[:, :, :].rearrange("p c (k2 k1) -> p k1 c k2", k1=64)[:, g4 * 4:(g4 + 1) * 4, :, :],
                            in_=ps[b][:, :].rearrange("p (i c k) -> p i c k", i=4, c=2), func=AF.Copy), r=[f'ps{b}'], w=['Xo'])
                    else:
                        P.add('dve', lambda e, g4=g4, b=b: e.tensor_copy(
                            Xo[:, :, :].rearrange("p c (k2 k1) -> p k1 c k2", k1=64)[:, g4 * 4:(g4 + 1) * 4, :, :],
                            ps[b][:, :].rearrange("p (i c k) -> p i c k", i=4, c=2)), r=[f'ps{b}'], w=['Xo'], free_psum=True)
                P.add('sp', lambda e, cc=cc: e.dma_start(out=gsc[s, cc:cc + 3:2, :, CTXL:TOK].rearrange("c p t -> p c t"), in_=Xo[:, :, :]),
                      r=['Xo'], chan='st0')
            if not last_layer:
                P.add('sp', lambda e: e.dma_start(out=uc[:], in_=ufsc[s, :, :, 0:CTXL].rearrange("c p t -> p c t")), r=[('ufsc', s)], w=['uc'], chan='ld1')
                b = fbank()
                def tc(e, b=b):
                    for nc_ in range(2):
                        for cc in range(2):
                            e.transpose(psb16[b][:, nc_ * 256 + cc * 128: nc_ * 256 + (cc + 1) * 128], uc[:, cc, nc_ * 128:(nc_ + 1) * 128], ident[:])
                P.add('pe', tc, r=['uc', 'ident'], w=[f'ps{b}'])
                P.add('act', lambda e, b=b: e.activation(out=uct[:, :, :], in_=psb16[b][:, 0:512].rearrange("p (n c) -> p n c", n=2), func=AF.Copy), r=[f'ps{b}'], w=['uct'])
                for cc in range(2):
                    b = fbank()
                    def dc(e, b=b, cc=cc):
                        for nc_ in range(2):
                            e.matmul(ps[b][:, :], uct[:, nc_, cc * 128:(cc + 1) * 128], d256[:, nc_, :], start=(nc_ == 0), stop=(nc_ == 1))
                    P.add('pe', dc, r=['uct', 'd256'], w=[f'ps{b}'])
                    P.add('act', lambda e, b=b, cc=cc: e.activation(out=xc[:, cc, :, :], in_=ps[b][:, :].rearrange("p (c k) -> p c k", c=2), func=AF.Copy), r=[f'ps{b}'], w=[('xc', cc)])
                for cc in range(2):
                    P.add('sp', lambda e, cc=cc: e.dma_start(out=gsc[s, cc:cc + 3:2, :, 0:CTXL].rearrange("c p t -> p c t"), in_=xc[:, cc, :, :]),
                          r=[('xc', cc)], chan='st1')
        if debug and l == 0 and os.environ.get('DBG_DUMPF'):
            P.barrier()
            dd = dbg_out("gsc", [8, 128, TOK], BF16)
            for c_ in range(8):
                P.add('sp', lambda e, dd=dd, c_=c_: e.dma_start(out=dd[c_], in_=gsc[0, c_]), chan='st1')
        P.barrier()
        A.pop()
        if stop_after == ('F', l):
            break

        A.push()
        weffc = A.alloc("weffc", [128, 10, D], BF16)
        g1bc = A.alloc("g1bc", [128, D], F32)
        KT = A.alloc("KT", [96, 4, TOK], BF16)
        Vg = A.alloc("Vg", [128, 34, 384], BF16)
        Qt = [A.alloc("Qt", [96, 4, 512], BF16) for _ in range(2)]
        Gt = A.alloc("Gt", [128, 8, 512], BF16)
        xtb = A.alloc("xtb", [128, 4, D], F32)
        attnT = [A.alloc("attnT", [128, 2, 512], BF16) for _ in range(2)]
        ptb = [A.alloc("ptb", [128, 2, 512], BF16) for _ in range(3)]
        rcb = A.alloc("rcb", [128, 512], F32)
        accs = [A.alloc("accs", [128, 512], F32) for _ in range(2)]
        wtmp = A.alloc("wtmp", [128, D], F32)
        VCOL = [0, 64, 192, 256]
        ctr = {'pt': 0, 'sg': 0}
        b1tiles = []
        for s in range(int(os.environ.get('DBG_NS', '2'))):
            for ti in range(9):
                if ti == 0 and last_layer:
                    continue
                b1tiles.append((s, ti))

        def b1info(s, ti):
            is_ctx = (ti == 0)
            ntok = CTXL if is_ctx else 512
            t0 = 0 if is_ctx else CTXL + (ti - 1) * 512
            return is_ctx, ntok, t0

        def load_q(idx):
            s, ti = b1tiles[idx]
            is_ctx, ntok, t0 = b1info(s, ti)
            P.add('sp', lambda e: e.dma_start(out=Qt[idx % 2][:, :, 0:ntok], in_=qsc[s, :, :, t0:t0 + ntok].rearrange("h p t -> p h t")),
                  r=[('qsc', s)], w=[('Qt', idx % 2)], chan='ld3')

        def load_gx(idx):
            s, ti = b1tiles[idx]
            is_ctx, ntok, t0 = b1info(s, ti)
            P.add('sp', lambda e: e.dma_start(out=Gt[:, :, 0:ntok], in_=gsc[s, :, :, t0:t0 + ntok].rearrange("c p t -> p c t")),
                  r=[('gsc', s)], w=['Gt'], chan='ld4')
            for j in range(ntok // 128):
                P.add('sp', lambda e, j=j: e.dma_start(out=xtb[:, j, :], in_=src_rows(l, s, t0 + j * 128, 128)),
                      w=[('xtb', j)], chan=f'xl{j % 2}')

        def wout_chunks(idx):
            s, ti = b1tiles[idx]
            is_ctx, ntok, t0 = b1info(s, ti)
            nsub = ntok // 128
            ab = idx % 2
            for j in range(nsub):
                for half in range(2):
                    def wo(e, j=j, half=half):
                        for c_ in range(10):
                            lh = Gt[:, c_, j * 128:(j + 1) * 128] if c_ < 8 else attnT[ab][:, c_ - 8, j * 128:(j + 1) * 128]
                            e.matmul(psg[3][:, half * 512:(half + 1) * 512], lh, weffc[:, c_, half * 512:(half + 1) * 512], start=(c_ == 0), stop=(c_ == 9))
                    P.add('pe', wo, r=['Gt'] + [('attnT', ab, a_, b_) for a_ in range(2) for b_ in range(2)] + [('weffc', c_) for c_ in range(10)], w=['psg3'])
                    yield
                P.add('dve', lambda e: e.tensor_copy(wtmp[:, :], psg[3][:, :]), r=['psg3'], w=['wtmp'], free_psum=True)
                P.add('dve', lambda e, j=j: e.tensor_tensor(xtb[:, j, :], wtmp[:, :], xtb[:, j, :], ALU.add),
                      r=['wtmp', ('xtb', j)], w=[('xtb', j)])
            P.add('sp', lambda e: e.dma_start(out=xs1[s, t0:t0 + ntok, :].rearrange("(j p) d -> p j d", p=128), in_=xtb[:, 0:nsub, :]),
                  r=[('xtb', j) for j in range(nsub)], chan='st2')
            yield

        cur = {'s': None, 'm': None}
        pending_wout = None
        load_q(0)
        for idx, (s, ti) in enumerate(b1tiles):
            is_ctx, ntok, t0 = b1info(s, ti)
            m = 2 if is_ctx else s
            nkp = 1 if is_ctx else 17
            ab_ = idx % 2
            if cur['s'] != s or cur['m'] != m:
                if pending_wout is not None:
                    for _ in pending_wout:
                        pass
                    pending_wout = None
            if cur['s'] != s:
                cur['s'] = s
                for h_ in range(4):
                    P.add('sp', lambda e, h_=h_: e.dma_start(out=KT[:, h_, :], in_=ksc[s, h_]), w=[('KT', h_)], chan=f'ldk{h_}')
                for v_ in range(2):
                    P.add('sp', lambda e, v_=v_: e.dma_start(out=Vg[:, v_ * 17:(v_ + 1) * 17, :],
                                                            in_=vsc[s, v_ * 2176:(v_ + 1) * 2176, :].rearrange("(c p) f -> p c f", p=128)),
                          w=[('Vg', v_)], chan=f'ldv{v_}')
            if cur['m'] != m:
                cur['m'] = m
                P.add('sp', lambda e, m=m: e.dma_start(out=g1bc[:], in_=modsc[l, m:m + 1, 2 * D:3 * D].to_broadcast([128, D])),
                      r=[('modsc', l)], w=['g1bc'], chan='ld2')
                for c_ in range(10):
                    P.add('dve', lambda e, c_=c_: e.tensor_tensor(weffc[:, c_, :], weff[:, c_, :], g1bc[:], ALU.mult),
                          r=[('weff', c_), 'g1bc'], w=[('weffc', c_)])
            if idx + 1 < len(b1tiles):
                load_q(idx + 1)
            tb = idx % 2
            npairs_total = 4 * nkp
            every = max(1, npairs_total // 9)
            pcount = 0
            for hp in range(2):
                accb = []
                for hh in range(2):
                    h = hp * 2 + hh
                    ab = 4 + hh
                    accb.append(ab)
                    vc = VCOL[h]

                    def do_s(kp, h=h):
                        g = ctr['sg'] % 2
                        ctr['sg'] += 1
                        def f(e, kp=kp, g=g):
                            for i2 in range(2):
                                kc = kp * 2 + i2
                                e.matmul(psg[g][:, i2 * 512:i2 * 512 + ntok], KT[0:96, h, kc * 128:(kc + 1) * 128], Qt[tb][0:96, h, 0:ntok],
                                         start=True, stop=True)
                        P.add('pe', f, r=[('KT', h), ('Qt', tb)], w=[f'psg{g}'])
                        pi = ctr['pt'] % 3
                        ctr['pt'] += 1
                        P.add('act', lambda e, g=g, pi=pi: e.activation(out=ptb[pi][:, :, 0:ntok], in_=psg[g].rearrange("p (b n) -> p b n", b=2)[:, :, 0:ntok],
                                                                       func=AF.Exp, scale=SCALE), r=[f'psg{g}'], w=[('ptb', pi)])
                        return pi

                    def do_pv(kp, pi, ab=ab, vc=vc):
                        def f(e, kp=kp, pi=pi):
                            for i2 in range(2):
                                kc = kp * 2 + i2
                                e.matmul(ps[ab][:, 0:ntok], Vg[:, kc, vc:vc + 128], ptb[pi][:, i2, 0:ntok],
                                         start=(kc == 0), stop=(kc == 2 * nkp - 1))
                        P.add('pe', f, r=[('Vg', 0 if kp * 2 + 1 < 17 else 1), ('Vg', 0 if kp * 2 < 17 else 1), ('ptb', pi)], w=[f'ps{ab}'])
                    pend = []
                    for kp in range(nkp):
                        pi = do_s(kp)
                        pend.append((kp, pi))
                        if len(pend) > 1:
                            do_pv(*pend.pop(0))
                        pcount += 1
                        if pending_wout is not None and pcount % every == 0:
                            if next(pending_wout, 'done') == 'done':
                                pending_wout = None
                    while pend:
                        do_pv(*pend.pop(0))
                ae, ao = accb
                P.add('dve', lambda e, ae=ae: e.tensor_copy(accs[0][:, 0:ntok], ps[ae][:, 0:ntok]), r=[f'ps{ae}'], w=[('accs', 0)], free_psum=True)
                P.add('dve', lambda e, ao=ao: e.tensor_copy(accs[1][:, 0:ntok], ps[ao][:, 0:ntok]), r=[f'ps{ao}'], w=[('accs', 1)], free_psum=True)
                P.add('dve', lambda e: e.reciprocal(rcb[0:64, 0:ntok], accs[0][64:128, 0:ntok]), r=[('accs', 0)], w=[('rcb', 0)])
                P.add('dve', lambda e, hp=hp: e.tensor_tensor(attnT[ab_][0:64, hp, 0:ntok], accs[0][0:64, 0:ntok], rcb[0:64, 0:ntok], ALU.mult),
                      r=[('accs', 0), ('rcb', 0)], w=[('attnT', ab_, hp, 0)])
                P.add('dve', lambda e: e.reciprocal(rcb[64:128, 0:ntok], accs[1][0:64, 0:ntok]), r=[('accs', 1)], w=[('rcb', 1)])
                P.add('dve', lambda e, hp=hp: e.tensor_tensor(attnT[ab_][64:128, hp, 0:ntok], accs[1][64:128, 0:ntok], rcb[64:128, 0:ntok], ALU.mult),
                      r=[('accs', 1), ('rcb', 1)], w=[('attnT', ab_, hp, 1)])
            if pending_wout is not None:
                for _ in pending_wout:
                    pass
            load_gx(idx)
            pending_wout = wout_chunks(idx)
        if pending_wout is not None:
            for _ in pending_wout:
                pass
        if debug and l == 0 and os.environ.get('DBG_DUMPB1'):
            P.barrier()
            dd = dbg_out("xs1", [TOK, D], F32)
            for c_ in range(4):
                P.add('sp', lambda e, dd=dd, c_=c_: e.dma_start(out=dd[c_ * 1088:(c_ + 1) * 1088, :], in_=xs1[0, c_ * 1088:(c_ + 1) * 1088, :]), chan='st1')
        P.barrier()
        A.pop()
        if stop_after == ('B1', l):
            break

        A.pop()
        A.push()
        w1b = A.alloc("w1b", [128, 8, 4 * D], BF16)
        w2b = A.alloc("w2b", [128, 32, D], BF16)
        nb2 = dict(
            xt=[A.alloc("xs2b", [128, D], F32) for _ in range(2)],
            xn=[A.alloc("xn2", [128, D], BF16) for _ in range(2)],
            hT=[A.alloc("hT2", [128, 8, 512], BF16)],
            ssq=[A.alloc("ssq2", [128, 4], F32)],
            junk=None,
        )
        xr = [A.alloc("xr2", [128, D], F32) for _ in range(2)]
        uT = A.alloc("uT", [128, 32, 512], BF16)
        rtb = [A.alloc("rtb", [128, 512], F32) for _ in range(2)]
        g2bc = A.alloc("g2bc", [128, D], BF16)
        fngbc = A.alloc("fngbc", [128, D], F32)
        ssf = A.alloc("ssf", [128, 4], F32)
        for k in range(8):
            P.add('pool', lambda e, k=k: e.dma_start(out=w1b[:, k, :], in_=mlp_w1[l, k * 128:(k + 1) * 128, :]), w=[('w1b', k)],
                  chan='wa' if k % 2 == 0 else 'wb')
        for j4 in range(8):
            P.add('pool', lambda e, j4=j4: e.dma_start(out=w2b[:, j4 * 4:(j4 + 1) * 4, :],
                                                      in_=mlp_w2[l, j4 * 512:(j4 + 1) * 512, :].rearrange("(j p) d -> p j d", p=128)),
                  w=[('w2b', j4)], chan='wa' if j4 % 2 == 0 else 'wb')
        if last_layer:
            P.add('sp', lambda e: e.dma_start(out=fngbc[:], in_=fng_in.to_broadcast([128, D])), w=['fngbc'], chan='ld2')
        ctr2 = {'r': 0}
        b2tiles = []
        for s in range(int(os.environ.get('DBG_NS', '2'))):
            for ti in range(9):
                if ti == 0 and last_layer:
                    continue
                b2tiles.append((s, ti))

        def b2info(s, ti):
            is_ctx = (ti == 0)
            ntok = CTXL if is_ctx else 512
            t0 = 0 if is_ctx else CTXL + (ti - 1) * 512
            return is_ctx, ntok, t0, (2 if is_ctx else s)

        def b2norm(idx):
            s, ti = b2tiles[idx]
            is_ctx, ntok, t0, m = b2info(s, ti)
            return norm_stage(nb2, l, s, t0, ntok, scB[l], modT[l][:, 72:96], m, lambda l_, s_, t_, n_: xs1[s_, t_:t_ + n_, :], split='stream')

        NB = {0: b2norm(0)}
        NB[0][1]()
        NB[0][2]()
        cur_m = None
        for idx, (s, ti) in enumerate(b2tiles):
            is_ctx, ntok, t0, m = b2info(s, ti)
            nsub = ntok // 128
            if cur_m != m:
                cur_m = m
                P.add('pool', lambda e, m=m: e.dma_start(out=g2bc[:], in_=modsc[l, m:m + 1, 5 * D:6 * D].to_broadcast([128, D])),
                      r=[('modsc', l)], w=['g2bc'], chan='wa')
            for jf in range(32):
                b = nb()
                def m1(e, jf=jf, b=b):
                    for k in range(8):
                        e.matmul(ps[b][:, 0:ntok], w1b[:, k, jf * 128:(jf + 1) * 128], nb2['hT'][0][:, k, 0:ntok], start=(k == 0), stop=(k == 7))
                P.add('pe', m1, r=[('w1b', k) for k in range(8)] + [('hT', 0, k) for k in range(8)], w=[f'ps{b}'])
                ri = ctr2['r'] % 2
                ctr2['r'] += 1
                P.add('act', lambda e, b=b, ri=ri: e.activation(out=rtb[ri][:, 0:ntok], in_=ps[b][:, 0:ntok], func=AF.Relu),
                      r=[f'ps{b}'], w=[('rtb', ri)])
                P.add('pool' if jf % 4 == 0 else 'dve', lambda e, jf=jf, ri=ri: e.tensor_tensor(uT[:, jf, 0:ntok], rtb[ri][:, 0:ntok], rtb[ri][:, 0:ntok], ALU.mult),
                      r=[('rtb', ri)], w=[('uT', jf)])
            if idx + 1 < len(b2tiles):
                NB[idx + 1] = b2norm(idx + 1)
                NB[idx + 1][1]()
            for j in range(nsub):
                xj = xr[j % 2]
                P.add('sp', lambda e, j=j, xj=xj: e.dma_start(out=xj[:, :], in_=xs1[s, t0 + j * 128:t0 + (j + 1) * 128, :]), w=[('xr', j % 2)], chan=f'xr{j % 2}')
                for half in range(2):
                    ob = nb()
                    def m2(e, j=j, half=half, ob=ob):
                        for jf in range(32):
                            e.matmul(ps[ob][:, :], uT[:, jf, j * 128:(j + 1) * 128], w2b[:, jf, half * 512:(half + 1) * 512], start=(jf == 0), stop=(jf == 31))
                    P.add('pe', m2, r=[('uT', jf) for jf in range(32)] + [('w2b', j4) for j4 in range(8)], w=[f'ps{ob}'])
                    ri = ctr2['r'] % 2
                    ctr2['r'] += 1
                    P.add('dve', lambda e, ob=ob, ri=ri: e.tensor_copy(rtb[ri][:, :], ps[ob][:, :]), r=[f'ps{ob}'], w=[('rtb', ri)], free_psum=True)
                    P.add('dve', lambda e, half=half, ri=ri: e.tensor_tensor(rtb[ri][:, :], rtb[ri][:, :], g2bc[:, half * 512:(half + 1) * 512], ALU.mult),
                          r=[('rtb', ri), 'g2bc'], w=[('rtb', ri)])
                    P.add('dve', lambda e, xj=xj, half=half, ri=ri, j=j: e.tensor_tensor(xj[:, half * 512:(half + 1) * 512], rtb[ri][:, :], xj[:, half * 512:(half + 1) * 512], ALU.add),
                          r=[('rtb', ri), ('xr', j % 2)], w=[('xr', j % 2)])
                if j == 0 and idx + 1 < len(b2tiles):
                    NB[idx + 1][2]()
                if last_layer:
                    P.add('act', lambda e, j=j, xj=xj: e.activation(out=rtb[0][:, :].bitcast(BF16), in_=xj[:, :], func=AF.Square, accum_out=ssf[:, j:j + 1]),
                          r=[('xr', j % 2)], w=[('rtb', 0), ('ssf', j)])
                    rsqrt_chain(ssf[:, j:j + 1], ssf[:, j:j + 1], 1.0 / D, ('ssf', j), ('ssf', j))
                    P.add('dve', lambda e, j=j, xj=xj: e.scalar_tensor_tensor(xj[:, :], xj[:, :], ssf[:, j:j + 1], fngbc[:], ALU.mult, ALU.mult),
                          r=[('xr', j % 2), ('ssf', j), 'fngbc'], w=[('xr', j % 2)])
                    P.add('sp', lambda e, j=j, xj=xj: e.dma_start(out=y_out[s, t0 - CTXL + j * 128:t0 - CTXL + (j + 1) * 128, :], in_=xj[:, :]),
                          r=[('xr', j % 2)], chan=f'so{j % 2}')
                else:
                    P.add('sp', lambda e, j=j, xj=xj: e.dma_start(out=xs2[s, t0 + j * 128:t0 + (j + 1) * 128, :], in_=xj[:, :]),
                          r=[('xr', j % 2)], chan=f'so{j % 2}')
        if debug and l == 0 and os.environ.get('DBG_DUMPB2'):
            P.barrier()
            dd = dbg_out("xs2", [TOK, D], F32)
            for c_ in range(4):
                P.add('sp', lambda e, dd=dd, c_=c_: e.dma_start(out=dd[c_ * 1088:(c_ + 1) * 1088, :], in_=xs2[0, c_ * 1088:(c_ + 1) * 1088, :]), chan='st1')
        P.barrier()
        A.pop()
        if stop_after == ('B2', l):
            break

    P.barrier()
    P.emit()
    print('arena peak', A.peak, 'instr', {e: len(P.streams[e]) for e in ENGS})
    return nc, dbg


def prep_inputs(inputs):
    cst = _get_consts()
    f = lambda a: np.ascontiguousarray(np.asarray(a, dtype=np.float32))
    x = f(inputs['x']); c = f(inputs['c']); ctx = f(inputs['ctx']); c_ctx = f(inputs['c_ctx'])
    shared = {k: f(inputs[k]) for k in ('ada_w', 'ada_b', 'w_in', 'fourier_w', 'pool_w', 'w_uq', 'w_ukv', 'w_out',
                                        'mlp_w1', 'mlp_w2')}
    shared['fng'] = f(inputs['final_norm_g']).reshape(1, D)
    for k, v in cst.items():
        shared[k] = v
    vec_base = np.zeros((128, NVEC), np.float32)
    for l in range(DEPTH):
        vec_base[:, VEC_COLS[f'n1g{l}']:VEC_COLS[f'n1g{l}'] + 8] = _chunked(inputs['norm1_g'][l])
        vec_base[:, VEC_COLS[f'n2g{l}']:VEC_COLS[f'n2g{l}'] + 8] = _chunked(inputs['norm2_g'][l])
        vec_base[:, VEC_COLS[f'qng{l}']:VEC_COLS[f'qng{l}'] + 2] = _chunked(inputs['q_norm_g'][l])
        vec_base[:, VEC_COLS[f'kvng{l}']:VEC_COLS[f'kvng{l}'] + 1] = _chunked(inputs['kv_norm_g'][l])
        cw = np.asarray(inputs['conv_w'][l], np.float32)
        for tap in range(3):
            vec_base[:, VEC_COLS[f'convw{l}'] + tap * 2:VEC_COLS[f'convw{l}'] + tap * 2 + 2] = _chunked(cw[tap])
        vec_base[:, VEC_COLS[f'pscale{l}']:VEC_COLS[f'pscale{l}'] + 2] = _chunked(inputs['pool_scale'][l])
    invw = np.zeros((128, 2), np.float32)
    invw[:64, 0] = 1 / 2; invw[64:, 0] = 1 / 4; invw[:64, 1] = 1 / 8; invw[64:, 1] = 1 / 16
    vec_base[:, VEC_COLS['invw']:VEC_COLS['invw'] + 2] = invw
    in_maps = []
    for i in range(NCORES):
        m = dict(shared)
        m['x'] = x[2 * i:2 * i + 2]
        m['ctx'] = ctx[2 * i:2 * i + 2]
        vb = vec_base.copy()
        cm = np.stack([c[2 * i], c[2 * i + 1], c_ctx], axis=1)
        vb[:, VEC_COLS['cT']:VEC_COLS['cT'] + 24] = cm.reshape(8, 128, 3).transpose(1, 0, 2).reshape(128, 24)
        m['vecs'] = vb
        in_maps.append(m)
    return in_maps


_NC_CACHE = {}


def kernel(**inputs):
    in_maps = prep_inputs(inputs)
    if 'nc' not in _NC_CACHE:
        _NC_CACHE['nc'] = build_program()[0]
    nc = _NC_CACHE['nc']
    res = run_bass_kernel_spmd(nc, in_maps, core_ids=list(range(NCORES)))
    out = np.concatenate([np.asarray(r['y'], dtype=np.float32) for r in res.results], axis=0)
    return out
```

```python
import math
import os
import numpy as np
import concourse.bass as bass
import concourse.mybir as mybir
from concourse.bass_utils import run_bass_kernel_spmd

F32 = mybir.dt.float32
BF16 = mybir.dt.bfloat16
AF = mybir.ActivationFunctionType
ALU = mybir.AluOpType

NCORES = 8
D = 1024
SEQ = 4096
CTXL = 256
TOK = SEQ + CTXL
DEPTH = 2
INC = 1696
EPS = 1e-6
SCALE = 1.0 / math.sqrt(96.0)
ENGS = ['pe', 'act', 'dve', 'pool', 'sp']
SAME_ENG_SYNC = True


class _Rec:
    def __init__(self):
        self.calls = []

    def __getattr__(self, name):
        def m(*a, **k):
            self.calls.append((name, a, k))
            return len(self.calls) - 1
        return m


class Prog:
    def __init__(self, nc):
        self.nc = nc
        self.streams = {e: [] for e in ENGS}
        self.res = {}
        self.waited = {e: {} for e in ENGS}
        self.chan_count = {}
        self.signal = set()

    def _dep(self, eng, tok, waits, nosame=False):
        if tok is None:
            return
        if tok[0] == 'e':
            _, e2, idx = tok
            if e2 == eng and (eng in ('pe', 'sp') or not SAME_ENG_SYNC or nosame):
                return
            key = ('e', e2)
            if self.waited[eng].get(key, -1) >= idx:
                return
            self.waited[eng][key] = idx
            waits.append(tok)
            self.signal.add((e2, idx))
        else:
            _, ch, n = tok
            key = ('d', ch)
            if self.waited[eng].get(key, 0) >= n:
                return
            self.waited[eng][key] = n
            waits.append(tok)

    def add(self, eng, fn, r=(), w=(), chan=None, ndma=1, free_psum=False, psum_ap=False):
        waits = []
        if free_psum or psum_ap:
            w = list(w) + ['!psum2']
        if eng in ('act', 'dve') and not free_psum:
            def _isps(k):
                return (isinstance(k, str) and k.startswith('ps')) or (isinstance(k, tuple) and k and k[0] == 'pst')
            if any(_isps(k) for k in r) or any(_isps(k) for k in w):
                w = list(w) + ['!psum']
        for k in r:
            st = self.res.get(k)
            if st:
                self._dep(eng, st[0], waits)
        for k in w:
            st = self.res.get(k)
            if st:
                ns = isinstance(k, str) and k.startswith('!')
                self._dep(eng, st[0], waits, ns)
                for t in st[1].values():
                    self._dep(eng, t, waits, ns)
        idx = len(self.streams[eng])
        if chan is not None:
            n0 = self.chan_count.get(chan, 0)
            if n0 > 0:
                self._dep(eng, ('d', chan, n0), waits)
            n1 = n0 + ndma
            self.chan_count[chan] = n1
            tok = ('d', chan, n1)
            rk = ('d', chan)
        else:
            tok = ('e', eng, idx)
            rk = ('e', eng)
        for k in r:
            self.res.setdefault(k, [None, {}])[1][rk] = tok
        for k in w:
            self.res[k] = [tok, {}]
        rec = _Rec()
        fn(rec)
        assert rec.calls
        if chan is not None:
            assert len(rec.calls) == ndma, (len(rec.calls), ndma)
        self.streams[eng].append(dict(fn=rec.calls, waits=waits, chan=chan, idx=idx, ndma=ndma))

    def barrier(self):
        toks = []
        for e in ENGS:
            if e == 'sp':
                continue
            real = [op['idx'] for op in self.streams[e] if op['fn'] is not None and op['chan'] is None]
            if real:
                toks.append(('e', e, real[-1]))
        for ch, n in self.chan_count.items():
            toks.append(('d', ch, n))
        for e in ENGS:
            waits = []
            for t in toks:
                if t[0] == 'e' and t[1] == e:
                    if e in ('pe', 'sp') or not SAME_ENG_SYNC:
                        continue
                self._dep(e, t, waits)
            self.streams[e].append(dict(fn=None, waits=waits, chan=None, idx=len(self.streams[e]), ndma=0))
        self.res = {}

    def emit(self):
        nc = self.nc
        rank = {}
        for e in ENGS:
            c = 0
            for op in self.streams[e]:
                if op['chan'] is None and (e, op['idx']) in self.signal:
                    if op['fn'] is None:
                        raise RuntimeError("signal on empty op")
                    c += 1
                    rank[(e, op['idx'])] = c
        from contextlib import ExitStack
        with ExitStack() as es:
            esem = {e: es.enter_context(nc.semaphore("s_" + e)) for e in ENGS if e != 'sp'}
            csem = {ch: es.enter_context(nc.semaphore("c_" + ch)) for ch in self.chan_count}
            block = es.enter_context(nc.Block())

            def make_body(e):
                def body(eng):
                    for op in self.streams[e]:
                        for tok in op['waits']:
                            if tok[0] == 'e':
                                eng.wait_ge(esem[tok[1]], rank[(tok[1], tok[2])])
                            else:
                                eng.wait_ge(csem[tok[1]], 16 * tok[2])
                        if op['fn'] is None:
                            continue
                        lst = [getattr(eng, nm_)(*a_, **k_) for (nm_, a_, k_) in op['fn']]
                        if op['chan'] is not None:
                            for i_ in lst:
                                i_.then_inc(csem[op['chan']], 16)
                        elif (e, op['idx']) in rank:
                            lst[-1].then_inc(esem[e], 1)
                return body
            block.tensor(make_body('pe'))
            block.scalar(make_body('act'))
            block.vector(make_body('dve'))
            block.gpsimd(make_body('pool'))
            block.sync(make_body('sp'))


class Arena:
    def __init__(self, nc, base, limit):
        self.nc = nc
        self.off = base
        self.limit = limit
        self.stack = []
        self.n = 0
        self.peak = base

    def alloc(self, name, shape, dtype):
        per = int(np.prod(shape[1:])) * (4 if dtype == F32 else 2)
        per = (per + 63) // 64 * 64
        assert self.off + per <= self.limit, f"SBUF arena overflow at {name}: {self.off}+{per}>{self.limit}"
        self.n += 1
        t = self.nc.alloc_sbuf_tensor_at(f"{name}_{self.n}", list(shape), dtype, offset=self.off)
        self.off += per
        self.peak = max(self.peak, self.off)
        return t

    def push(self):
        self.stack.append(self.off)

    def pop(self):
        self.off = self.stack.pop()


VEC_COLS = {}


def _vec_layout():
    off = 0
    def put(name, n):
        nonlocal off
        VEC_COLS[name] = off
        off += n
    put('cT', 24)
    for l in range(DEPTH):
        put(f'n1g{l}', 8)
        put(f'n2g{l}', 8)
        put(f'qng{l}', 2)
        put(f'kvng{l}', 1)
        put(f'convw{l}', 6)
        put(f'pscale{l}', 2)
    put('invw', 2)
    return off


NVEC = _vec_layout()


def _chunked(v):
    v = np.asarray(v, np.float32)
    return np.ascontiguousarray(v.reshape(-1, 128).T)


def _const_tables():
    t = {}
    t['ident'] = np.eye(128, dtype=np.float32)
    n1 = np.arange(64)[:, None].astype(np.float64)
    k1 = np.arange(64)[None, :].astype(np.float64)
    ang = 2 * np.pi * n1 * k1 / 64.0
    t['W1'] = np.concatenate([np.cos(ang), -np.sin(ang)], axis=1).astype(np.float32)
    n2 = np.arange(64).reshape(64, 1, 1).astype(np.float64)
    kk1 = np.arange(64).reshape(1, 64, 1).astype(np.float64)
    kk2 = np.arange(64).reshape(1, 1, 64).astype(np.float64)
    th = 2 * np.pi * n2 * (kk1 + 64 * kk2) / 4096.0
    cs, sn = np.cos(th), np.sin(th)
    m2 = np.zeros((2, 64, 64, 2, 64), np.float64)
    m2[0, :, :, 0, :] = cs
    m2[0, :, :, 1, :] = -sn
    m2[1, :, :, 0, :] = sn
    m2[1, :, :, 1, :] = cs
    t['M2'] = m2.reshape(2, 64, 64 * 128).astype(np.float32)
    a = 2 * np.pi * np.arange(64)[:, None] * np.arange(64)[None, :] / 64.0
    bc = np.kron(np.eye(4), np.cos(a))
    bs = np.kron(np.eye(4), np.sin(a))
    t['Bcs'] = np.concatenate([bc, bs], axis=1).astype(np.float32)
    a = 2 * np.pi * np.arange(256)[:, None] * np.arange(256)[None, :] / 256.0
    t['D256'] = (4.0 * np.concatenate([np.cos(a), -np.sin(a)], axis=1)).astype(np.float32)
    rows = SEQ // 64
    row = np.repeat(np.arange(rows, dtype=np.float32), 64)
    col = np.tile(np.arange(64, dtype=np.float32), rows)
    inv = (10000.0 ** (-np.arange(0, 16, 2, dtype=np.float32) / 16.0)).astype(np.float32)
    ang_r = row[:, None] * inv[None, :]
    ang_c = col[:, None] * inv[None, :]
    angf = np.concatenate([ang_r, ang_r, ang_c, ang_c], axis=-1).astype(np.float32)
    rope = np.zeros((32, 2, TOK), np.float32)
    rope[:, 0, :CTXL] = 1.0
    rope[:, 0, CTXL:] = np.cos(angf).T
    rope[:, 1, CTXL:] = np.sin(angf).T
    t['rope'] = np.ascontiguousarray(np.tile(rope, (4, 1, 1)))
    pe_ = np.zeros((128, 2, 2, 16), np.float32)
    wins = (2, 4, 8, 16)
    for g, w in enumerate(wins):
        for si, n in enumerate((SEQ, CTXL)):
            tt = np.concatenate([np.arange(8), np.arange(n - 8, n)])
            lo = np.maximum(tt - w // 2, 0)
            hi = np.minimum(tt + w // 2 - 1, n - 1)
            cnt = (hi - lo + 1).astype(np.float32)
            pe_[(g % 2) * 64:(g % 2) * 64 + 64, g // 2, si, :] = (1.0 / cnt)[None, :]
    t['pedge'] = pe_.reshape(128, 64)
    return t


_CONSTS = None


def _get_consts():
    global _CONSTS
    if _CONSTS is None:
        _CONSTS = _const_tables()
    return _CONSTS


def build_program(debug=False, stop_after=None):
    nc = bass.Bass("TRN2", target_bir_lowering=False)
    P = Prog(nc)
    dbg = {}

    def din(name, shape, dt=F32):
        return nc.dram_tensor(name, list(shape), dt, kind="ExternalInput").ap()

    def dscr(name, shape, dt):
        return nc.dram_tensor(name, list(shape), dt, kind="Internal").ap()

    x_in = din("x", [2, SEQ, D])
    ctx_in = din("ctx", [2, CTXL, D])
    vecs_in = din("vecs", [128, NVEC])
    fng_in = din("fng", [1, D])
    ada_w = din("ada_w", [DEPTH, D, 6 * D])
    ada_b = din("ada_b", [DEPTH, 6 * D])
    w_in = din("w_in", [DEPTH, D, INC])
    fourier_w = din("fourier_w", [DEPTH, 256, 256])
    pool_w = din("pool_w", [DEPTH, 4, 64, 64])
    w_uq = din("w_uq", [DEPTH, 256, 384])
    w_ukv = din("w_ukv", [DEPTH, 128, 512])
    w_out = din("w_out", [DEPTH, D, D])
    mlp_w1 = din("mlp_w1", [DEPTH, D, 4 * D])
    mlp_w2 = din("mlp_w2", [DEPTH, 4 * D, D])
    c_ident = din("ident", [128, 128])
    c_W1 = din("W1", [64, 128])
    c_M2 = din("M2", [2, 64, 64 * 128])
    c_Bcs = din("Bcs", [256, 512])
    c_D256 = din("D256", [256, 512])
    c_rope = din("rope", [128, 2, TOK])
    c_pedge = din("pedge", [128, 64])

    y_out = nc.dram_tensor("y", [2, SEQ, D], F32, kind="ExternalOutput").ap()

    modsc = dscr("modsc", [DEPTH, 3, 6 * D], F32)
    xs1 = dscr("xs1", [2, TOK, D], F32)
    xs2 = dscr("xs2", [2, TOK, D], F32)
    gsc = dscr("gsc", [2, 8, 128, TOK], BF16)
    qsc = dscr("qsc", [2, 4, 96, TOK], BF16)
    ksc = dscr("ksc", [2, 4, 96, TOK], BF16)
    vsc = dscr("vsc", [2, TOK, 384], BF16)
    ufsc = dscr("ufsc", [2, 2, 128, TOK], BF16)

    if debug:
        def dbg_out(name, shape, dt=F32):
            dbg[name] = nc.dram_tensor("dbg_" + name, list(shape), dt, kind="ExternalOutput").ap()
            return dbg[name]
    A = Arena(nc, 16896, 227328)

    psall = nc.alloc_psum_tensor("psall", [128, 4096], F32)
    ps = [psall[:, i * 512:(i + 1) * 512] for i in range(8)]
    psb16 = [psall[:, i * 512:(i + 1) * 512].bitcast(BF16) for i in range(8)]
    psg = [psall[:, g * 1024:(g + 1) * 1024] for g in range(4)]

    ident = A.alloc("ident", [128, 128], BF16)
    ones = A.alloc("ones", [128, 128], BF16)
    vecs = A.alloc("vecs", [128, NVEC], F32)
    modT = [A.alloc(f"modT{l}", [128, 144], F32) for l in range(DEPTH)]
    scA = [A.alloc(f"scA{l}", [128, 24], F32) for l in range(DEPTH)]
    scB = [A.alloc(f"scB{l}", [128, 24], F32) for l in range(DEPTH)]
    pedge = A.alloc("pedge", [128, 64], F32)
    epsc = A.alloc("epsc", [128, 1], F32)

    def V(name, n=1, off=0):
        c = VEC_COLS[name] + off
        return vecs[:, c:c + n]

    P.add('pool', lambda e: e.dma_start(out=ident[:], in_=c_ident), w=['ident'], chan='wa')
    P.add('sp', lambda e: e.dma_start(out=vecs[:], in_=vecs_in), w=['vecs'], chan='ld0')
    P.add('sp', lambda e: e.dma_start(out=pedge[:], in_=c_pedge), w=['pedge'], chan='ld1')
    P.add('dve', lambda e: e.memset(ones[:], 1.0), w=['ones'])
    P.add('dve', lambda e: e.memset(epsc[:], EPS), w=['epsc'])

    A.push()
    adaw = A.alloc("adaw", [128, 8, 6 * D], BF16)
    adab = A.alloc("adab", [1, 6 * D], BF16)
    sT = A.alloc("sT", [128, 24], BF16)
    modrow = A.alloc("modrow", [3, 6 * D], F32)
    P.add('act', lambda e: e.activation(out=sT[:], in_=V('cT', 24), func=AF.Silu), r=['vecs'], w=['sT'])
    for l in range(DEPTH):
        for k in range(8):
            P.add('pool', lambda e, l=l, k=k: e.dma_start(out=adaw[:, k, :], in_=ada_w[l, k * 128:(k + 1) * 128, :]),
                  w=[('adaw', k)], chan='wa')
        P.add('pool', lambda e, l=l: e.dma_start(out=adab[:], in_=ada_b[l:l + 1, :]), w=['adab'], chan='wb')
        def colform(e, l=l):
            ins = None
            for m in range(48):
                for k in range(8):
                    e.matmul(ps[0][:, m * 3:m * 3 + 3], adaw[:, k, m * 128:(m + 1) * 128], sT[:, k * 3:k * 3 + 3],
                             start=(k == 0), stop=False)
                ins = e.matmul(ps[0][:, m * 3:m * 3 + 3], adab[0:1, m * 128:(m + 1) * 128], ones[0:1, 0:3],
                               start=False, stop=True)
            return ins
        P.add('pe', colform, r=[('adaw', k) for k in range(8)] + ['adab', 'sT', 'ones'], w=['ps0'])
        P.add('dve', lambda e, l=l: e.tensor_copy(modT[l][:], ps[0][:, 0:144]), r=['ps0'], w=[('modT', l)])
        for blk in range(12):
            bank = 1 + blk % 2
            def rowform(e, blk=blk, bank=bank):
                for k in range(8):
                    e.matmul(ps[bank][0:3, :], sT[:, k * 3:k * 3 + 3], adaw[:, k, blk * 512:(blk + 1) * 512],
                             start=(k == 0), stop=False)
                return e.matmul(ps[bank][0:3, :], ones[0:1, 0:3], adab[0:1, blk * 512:(blk + 1) * 512],
                                start=False, stop=True)
            P.add('pe', rowform, r=[('adaw', k) for k in range(8)] + ['adab', 'sT', 'ones'], w=[f'ps{bank}'])
            P.add('act', lambda e, blk=blk, bank=bank: e.activation(out=modrow[:, blk * 512:(blk + 1) * 512],
                                                                  in_=ps[bank][0:3, :], func=AF.Copy),
                  r=[f'ps{bank}'], w=['modrow'])
        P.add('sp', lambda e, l=l: e.dma_start(out=modsc[l], in_=modrow[:]), r=['modrow'], w=[('modsc', l)], chan='st0')
        for (dst, gname, c0) in ((scA[l], f'n1g{l}', 24), (scB[l], f'n2g{l}', 96)):
            P.add('dve', lambda e, dst=dst, c0=c0, l=l: e.tensor_scalar(dst[:], modT[l][:, c0:c0 + 24], 1.0, None, ALU.add),
                  r=[('modT', l)], w=[('sc', l, c0)])
            P.add('dve', lambda e, dst=dst, gname=gname: e.tensor_tensor(
                dst[:].rearrange("p (k m) -> p k m", m=3), dst[:].rearrange("p (k m) -> p k m", m=3),
                V(gname, 8).unsqueeze(2).to_broadcast([128, 8, 3]), ALU.mult),
                r=[('sc', l, c0), 'vecs'], w=[('sc', l, c0)])
    if debug:
        d_modT = dbg_out("modT", [DEPTH, 128, 144])
        d_scA = dbg_out("scA", [DEPTH, 128, 24])
        for l in range(DEPTH):
            P.add('sp', lambda e, l=l: e.dma_start(out=d_modT[l], in_=modT[l][:]), r=[('modT', l)], chan='st1')
            P.add('sp', lambda e, l=l: e.dma_start(out=d_scA[l], in_=scA[l][:]), r=[('sc', l, 24)], chan='st1')
    P.barrier()
    A.pop()


    cnt = {'pb': 0}

    def nb():
        b = 4 + cnt['pb'] % 4
        cnt['pb'] += 1
        return b

    def src_rows(l, s, t0, n):
        if l == 0:
            if t0 < CTXL:
                return ctx_in[s, t0:t0 + n, :]
            return x_in[s, t0 - CTXL:t0 - CTXL + n, :]
        return xs2[s, t0:t0 + n, :]

    def rsqrt_chain(dst, src_ps, inv_n, key_src, key_dst, shape_p=128):
        P.add('dve', lambda e: e.tensor_scalar(dst, src_ps, inv_n, EPS, ALU.mult, ALU.add), r=[key_src], w=[key_dst])
        P.add('act', lambda e: e.activation(out=dst, in_=dst, func=AF.Sqrt), r=[key_dst], w=[key_dst])
        P.add('dve', lambda e: e.reciprocal(dst, dst), r=[key_dst], w=[key_dst])

    tilectr = {'n': 0}

    def norm_stage(nb_, l, s, t0, ntok, sc_t, sh_t, m, src_fn, defer=False, split=False):
        tb = tilectr['n'] % len(nb_['hT'])
        tilectr['n'] += 1
        nsub = ntok // 128
        xt = nb_['xt']
        xn = nb_['xn']
        nxn = len(xn)
        hT = nb_['hT'][tb]
        ssq = nb_['ssq'][tb]

        def tr_op(j):
            xb = j % nxn
            def tr(e, j=j, xb=xb):
                for k in range(8):
                    bank, slot = k // 2, k % 2
                    e.transpose(psb16[bank][:, slot * 512 + j * 128: slot * 512 + (j + 1) * 128],
                                xn[xb][:, k * 128:(k + 1) * 128], ident[:])
            P.add('pe', tr, r=[('xn', xb), 'ident'], w=[('pst', j)] + [f'ps{b}' for b in range(4)] if j == 0 else [('pst', j)])

        xlist = isinstance(xt, list)

        def xtj(j):
            return xt[j % len(xt)][:, :] if xlist else xt[:, j, :]

        def xkey(j):
            return ('xt', j % len(xt)) if xlist else ('xt', j)

        def load(j):
            src = src_fn(l, s, t0 + j * 128, 128)
            P.add('sp', lambda e, j=j, src=src: e.dma_start(out=xtj(j), in_=src), w=[xkey(j)], chan=f'xl{j % 2}')

        def square(j):
            xb = j % nxn
            junk = nb_['junk'] if nb_.get('junk') is not None else xn[xb]
            jkey = 'junk' if nb_.get('junk') is not None else ('xn', xb)
            P.add('act', lambda e, j=j, junk=junk: e.activation(out=junk[:], in_=xtj(j), func=AF.Square,
                                                                accum_out=ssq[:, j:j + 1]),
                  r=[xkey(j)], w=[jkey, ('ssq', tb, j)])

        def xnorm(j):
            xb = j % nxn
            P.add('dve', lambda e, j=j, xb=xb: e.tensor_scalar(xn[xb][:], xtj(j), ssq[:, j:j + 1], None, ALU.mult),
                  r=[xkey(j), ('ssq', tb, j)], w=[('xn', xb)])

        def loads_fn():
            for j in range(nsub):
                load(j)

        def compute_fn(gen=False):
            for j in range(nsub):
                square(j)
                if gen:
                    yield
            keys = [('ssq', tb, j) for j in range(nsub)]
            P.add('dve', lambda e: e.tensor_scalar(ssq[:, 0:nsub], ssq[:, 0:nsub], 1.0 / D, EPS, ALU.mult, ALU.add), r=keys, w=keys)
            P.add('act', lambda e: e.activation(out=ssq[:, 0:nsub], in_=ssq[:, 0:nsub], func=AF.Sqrt), r=keys, w=keys)
            P.add('dve', lambda e: e.reciprocal(ssq[:, 0:nsub], ssq[:, 0:nsub]), r=keys, w=keys)
            for j in range(nsub):
                xnorm(j)

        def part_b(gen=False):
            if defer or split is True:
                for j in range(nsub):
                    tr_op(j)
            for k in range(8):
                if gen and k > 0:
                    yield
                bank, slot = k // 2, k % 2
                src = psb16[bank][:, slot * 512: slot * 512 + ntok]
                scl = sc_t[:, k * 3 + m:k * 3 + m + 1]
                shf = sh_t[:, k * 3 + m:k * 3 + m + 1]
                rr = [('pst', j) for j in range(nsub)] + [('sc', l, 24), ('sc', l, 96), ('modT', l)]
                P.add('act', lambda e, k=k, src=src, scl=scl, shf=shf: e.activation(
                    out=hT[:, k, 0:ntok], in_=src, func=AF.Identity, scale=scl, bias=shf), r=rr, w=[('hT', tb, k)], psum_ap=True)

        def stream_fn():
            for j in range(nsub):
                load(j)
                square(j)
                rsqrt_chain(ssq[:, j:j + 1], ssq[:, j:j + 1], 1.0 / D, ('ssq', tb, j), ('ssq', tb, j))
                xnorm(j)
                tr_op(j)

        if split == 'stream':
            return tb, stream_fn, (lambda: [None for _ in part_b()])
        if split:
            return tb, loads_fn, compute_fn, part_b
        for j in range(nsub):
            load(j)
            square(j)
            rsqrt_chain(ssq[:, j:j + 1], ssq[:, j:j + 1], 1.0 / D, ('ssq', tb, j), ('ssq', tb, j))
            xnorm(j)
            if not defer:
                tr_op(j)
        if defer:
            return tb, (lambda: [None for _ in part_b()])
        for _ in part_b():
            pass
        return tb

    for l in range(DEPTH):
        last_layer = (l == DEPTH - 1)
        A.push()
        win = A.alloc("win", [128, 8, INC], BF16)
        wkr = A.alloc("wkr", [128, 8, 2, 32], BF16)
        wuqn = A.alloc("wuqn", [128, 2, 256], BF16)
        wuqp = A.alloc("wuqp", [128, 2, 128], BF16)
        wuqr = A.alloc("wuqr", [128, 2, 128], BF16)
        wukvn = A.alloc("wukvn", [128, 256], BF16)
        wukvv = A.alloc("wukvv", [128, 256], BF16)
        weff = A.alloc("weff", [128, 10, D], BF16)
        A.push()
        woutb = A.alloc("woutb", [128, 8, D], BF16)
        wfb = A.alloc("wfb", [128, 2, 256], BF16)
        bcs = A.alloc("bcs", [128, 2, 512], BF16)
        wuqf = A.alloc("wuqf", [128, 2, 384], F32)
        wukvf = A.alloc("wukvf", [128, 512], F32)
        poolP = A.alloc("poolP", [128, 2, 128], BF16)
        poolPT = A.alloc("poolPT", [128, 2, 128], BF16)
        wcT = A.alloc("wcT", [128, 2, 2, 256], BF16)

        for k in range(8):
            P.add('pool', lambda e, k=k: e.dma_start(out=win[:, k, :], in_=w_in[l, k * 128:(k + 1) * 128, :]),
                  w=[('win', k)], chan='wa' if k % 2 == 0 else 'wb')
        for k in range(8):
            P.add('pool', lambda e, k=k: e.dma_start(out=woutb[:, k, :], in_=w_out[l, k * 128:(k + 1) * 128, :]),
                  w=[('woutb', k)], chan='wa' if k % 2 == 0 else 'wb')
        P.add('pool', lambda e: e.dma_start(out=wfb[:], in_=fourier_w[l].rearrange("(c p) f -> p c f", p=128)),
              w=['wfb'], chan='wa')
        P.add('pool', lambda e: e.dma_start(out=bcs[:], in_=c_Bcs.rearrange("(c p) f -> p c f", p=128)),
              w=['bcs'], chan='wb')
        P.add('sp', lambda e: e.dma_start(out=wuqf[:], in_=w_uq[l].rearrange("(c p) f -> p c f", p=128)),
              w=['wuqf'], chan='ld0')
        P.add('sp', lambda e: e.dma_start(out=wukvf[:], in_=w_ukv[l]), w=['wukvf'], chan='ld1')
        P.add('dve', lambda e: e.memset(poolP[:], 0.0), w=['poolP'])
        for g in range(4):
            P.add('pool', lambda e, g=g: e.dma_start(out=poolP[(g % 2) * 64:(g % 2) * 64 + 64, g // 2, (g % 2) * 64:(g % 2) * 64 + 64],
                                                     in_=pool_w[l, g]), r=[], w=['poolP'], chan='wa')
        P.add('dve', lambda e: e.tensor_copy(wkr[:, :, 0, :], win[:, :, 1664:1696]), r=[('win', k) for k in range(8)], w=['wkr'])
        def krview(t, a, b):
            return t.rearrange("p k (h j) -> p k h j", h=2)[:, :, :, a:b]
        P.add('dve', lambda e: e.tensor_scalar(krview(wkr[:, :, 1, :], 0, 8), krview(win[:, :, 1664:1696], 8, 16),
                                               -1.0, None, ALU.mult), r=[('win', k) for k in range(8)], w=['wkr'])
        P.add('dve', lambda e: e.tensor_copy(krview(wkr[:, :, 1, :], 8, 16), krview(win[:, :, 1664:1696], 0, 8)),
              r=[('win', k) for k in range(8)], w=['wkr'])
        for kc in range(2):
            src4 = wuqf[:, kc, :].rearrange("p (h c) -> p h c", h=4)
            P.add('dve', lambda e, kc=kc, src4=src4: e.tensor_scalar(wuqn[:, kc, :].rearrange("p (h c) -> p h c", h=4), src4[:, :, 0:64],
                                                                    V(f'qng{l}', 1, kc), None, ALU.mult), r=['wuqf', 'vecs'], w=['wuqn'])
            P.add('dve', lambda e, kc=kc, src4=src4: e.tensor_scalar(wuqp[:, kc, :].rearrange("p (h c) -> p h c", h=4), src4[:, :, 64:96],
                                                                    V(f'qng{l}', 1, kc), None, ALU.mult), r=['wuqf', 'vecs'], w=['wuqp'])
            pv_ = lambda t, a, b, kc=kc: t[:, kc, :].rearrange("p (h f j) -> p h f j", h=4, f=2)[:, :, :, a:b]
            P.add('dve', lambda e, pv_=pv_: e.tensor_scalar(pv_(wuqr, 0, 8), pv_(wuqp, 8, 16), -1.0, None, ALU.mult), r=['wuqp'], w=['wuqr'])
            P.add('dve', lambda e, pv_=pv_: e.tensor_copy(pv_(wuqr, 8, 16), pv_(wuqp, 0, 8)), r=['wuqp'], w=['wuqr'])
        kv4 = wukvf[:, :].rearrange("p (h c) -> p h c", h=4)
        P.add('dve', lambda e: e.tensor_scalar(wukvn[:, :].rearrange("p (h c) -> p h c", h=4), kv4[:, :, 0:64], V(f'kvng{l}', 1), None, ALU.mult),
              r=['wukvf', 'vecs'], w=['wukvn'])
        P.add('dve', lambda e: e.tensor_scalar(wukvv[:, :].rearrange("p (h c) -> p h c", h=4), kv4[:, :, 64:128], V(f'kvng{l}', 1), None, ALU.mult),
              r=['wukvf', 'vecs'], w=['wukvv'])
        for (dst, srck) in ((4, 2), (5, 3), (8, 6), (9, 7)):
            P.add('act', lambda e, dst=dst, srck=srck: e.activation(out=weff[:, dst, :], in_=woutb[:, srck, :], func=AF.Copy),
                  r=[('woutb', srck)], w=[('weff', dst)])
        for cp in range(2):
            b = nb()
            def f1(e, cp=cp, b=b):
                ins = None
                for jc in range(2):
                    ins = e.matmul(ps[b][:, :], wfb[:, jc, cp * 128:(cp + 1) * 128], bcs[:, jc, :], start=(jc == 0), stop=(jc == 1))
                return ins
            P.add('pe', f1, r=['wfb', 'bcs'], w=[f'ps{b}'])
            P.add('act', lambda e, cp=cp, b=b: e.activation(out=wcT[:, :, cp, :], in_=ps[b][:, :].rearrange("p (t c) -> p t c", t=2),
                                                          func=AF.Copy, scale=1.0 / 512.0), r=[f'ps{b}'], w=['wcT'])
        for t in range(2):
            for cc in range(2):
                for half in range(2):
                    b = nb()
                    def f2(e, t=t, cc=cc, half=half, b=b):
                        ins = None
                        for cp in range(2):
                            ins = e.matmul(ps[b][:, :], wcT[:, t, cp, cc * 128:(cc + 1) * 128], woutb[:, cp, half * 512:(half + 1) * 512],
                                           start=(cp == 0), stop=(cp == 1))
                        return ins
                    P.add('pe', f2, r=['wcT', ('woutb', 0), ('woutb', 1)], w=[f'ps{b}'])
                    P.add('dve', lambda e, t=t, cc=cc, half=half, b=b: e.tensor_copy(weff[:, 2 * t + cc, half * 512:(half + 1) * 512], ps[b][:, :]),
                          r=[f'ps{b}'], w=[('weff', 2 * t + cc)])
        for cc in range(2):
            b = nb()
            P.add('pe', lambda e, cc=cc, b=b: e.transpose(psb16[b][:, 0:128], poolP[:, cc, :], ident[:]), r=['poolP', 'ident'], w=[f'ps{b}'])
            P.add('act', lambda e, cc=cc, b=b: e.activation(out=poolPT[:, cc, :], in_=psb16[b][:, 0:128], func=AF.Copy,
                                                          scale=V(f'pscale{l}', 1, cc)), r=[f'ps{b}', 'vecs'], w=['poolPT'])
            for half in range(2):
                b2 = nb()
                P.add('pe', lambda e, cc=cc, half=half, b2=b2: e.matmul(ps[b2][:, :], poolPT[:, cc, :], woutb[:, 4 + cc, half * 512:(half + 1) * 512],
                                                                       start=True, stop=True), r=['poolPT', ('woutb', 4 + cc)], w=[f'ps{b2}'])
                P.add('dve', lambda e, cc=cc, half=half, b2=b2: e.tensor_copy(weff[:, 6 + cc, half * 512:(half + 1) * 512], ps[b2][:, :]),
                      r=[f'ps{b2}'], w=[('weff', 6 + cc)])
        if debug and l == 0:
            d_weff = dbg_out("weff", [128, 10, D], BF16)
            P.add('sp', lambda e: e.dma_start(out=d_weff, in_=weff[:]), r=[('weff', c) for c in range(10)], chan='st1')
        P.barrier()
        A.pop()
        if stop_after == ('W', l):
            break

        A.push()
        nbuf = dict(
            xt=A.alloc("xt", [128, 4, D], F32),
            xn=[A.alloc("xn", [128, D], BF16) for _ in range(4)],
            hT=[A.alloc("hT", [128, 8, 512], BF16) for _ in range(2)],
            ssq=[A.alloc("ssq", [128, 4], F32) for _ in range(2)],
            junk=None,
        )
        uFt = [A.alloc("uFt", [128, 2, 512], BF16) for _ in range(2)]
        BW = 536
        bgb = [A.alloc("bgb", [128, 2, BW], BF16) for _ in range(2)]
        zb = [A.alloc("zb", [128, 2, BW], BF16) for _ in range(2)]
        ub = [A.alloc("ub", [128, 2, BW], BF16) for _ in range(2)]
        cgt = A.alloc("cgt", [128, 2, 512], BF16)
        ytmp = A.alloc("ytmp", [128, 520], F32)
        a2 = A.alloc("a2", [128, 2, BW], F32)
        a4 = A.alloc("a4", [128, 2, BW], F32)
        a8 = A.alloc("a8", [128, BW], F32)
        a16 = A.alloc("a16", [128, BW], F32)
        etmp = A.alloc("etmp", [128, 8], F32)
        sout = A.alloc("sout", [128, 2, 520], BF16)
        pout = A.alloc("pout", [128, 2, 520], BF16)
        cqT = A.alloc("cqT", [128, 3, 512], BF16)
        sqT = A.alloc("sqT", [128, 3, 512], BF16)
        cqn = A.alloc("cqn", [128, 3, 512], BF16)
        rqbc = A.alloc("rqbc", [128, 512], F32)
        rkvbc = A.alloc("rkvbc", [128, 512], F32)
        ropet = [A.alloc("ropet", [128, 2, 512], F32) for _ in range(2)]
        rt1 = A.alloc("rt1", [128, 512], F32)
        rt2 = A.alloc("rt2", [128, 512], F32)
        kt1 = A.alloc("kt1", [32, 512], F32)
        kt2 = A.alloc("kt2", [32, 512], F32)
        qn = A.alloc("qn", [128, 2, 512], BF16)
        qpe = A.alloc("qpe", [128, 512], BF16)
        kn = A.alloc("kn", [128, 2, 512], BF16)
        kpe = A.alloc("kpe", [32, 512], BF16)
        vo = A.alloc("vo", [128, 4, 384], BF16)
        P.add('dve', lambda e: e.memset(vo[:], 1.0), w=['vo'])
        WIN_ALL = [('win', k) for k in range(8)]
        HT = nbuf['hT']
        stc = {'n': 0}

        def stchan():
            stc['n'] += 1
            return f"sa{stc['n'] % 6}"

        def proj_fm(tb, ncols, ntok, b, lhs):
            def f(e):
                for k in range(8):
                    e.matmul(ps[b][0:ncols, 0:ntok], lhs(k), HT[tb][:, k, 0:ntok], start=(k == 0), stop=(k == 7))
            P.add('pe', f, r=WIN_ALL + ['wkr'] + [('hT', tb, k) for k in range(8)], w=[f'ps{b}'])

        def wcols(c0, n=128):
            return lambda k: win[:, k, c0:c0 + n]

        def tile_info(s, ti):
            is_ctx = (ti == 0)
            ntok = CTXL if is_ctx else 512
            t0 = 0 if is_ctx else CTXL + (ti - 1) * 512
            return is_ctx, ntok, t0

        def stageN(s, ti):
            is_ctx, ntok, t0 = tile_info(s, ti)
            m = 2 if is_ctx else s
            tb, loads_fn, compute_fn, part_b = norm_stage(nbuf, l, s, t0, ntok, scA[l], modT[l], m, src_rows, split=True)
            def rope_fn():
                P.add('sp', lambda e: e.dma_start(out=ropet[tb][:, :, 0:ntok], in_=c_rope[:, :, t0:t0 + ntok]), w=[('ropet', tb)], chan=f'rp{tb}')
            return dict(tb=tb, loads=loads_fn, compute=compute_fn, part_b=part_b, rope=rope_fn)

        def act_chain(dst, src_ps, inv_n, key_src, key_dst):
            P.add('act', lambda e: e.activation(out=dst, in_=src_ps, func=AF.Sqrt, scale=inv_n, bias=epsc[:, 0:1]), r=[key_src, 'epsc'], w=[key_dst])
            P.add('dve', lambda e: e.reciprocal(dst, dst), r=[key_dst], w=[key_dst])

        def stageP(s, ti, tb, mid_hook=None):
            is_ctx, ntok, t0 = tile_info(s, ti)
            nsub = ntok // 128
            first = is_ctx or ti == 1
            lastt = is_ctx or ti == 8
            full = not (is_ctx and last_layer)
            pb, pp = tb, 1 - tb
            for c3 in range(3):
                if c3 < 2 and not full:
                    continue
                b = nb()
                proj_fm(tb, 128, ntok, b, wcols(1280 + c3 * 128))
                P.add('act', lambda e, b=b, c3=c3: e.activation(out=cqT[:, c3, 0:ntok], in_=ps[b][:, 0:ntok], func=AF.Copy),
                      r=[f'ps{b}'], w=[('cqT', c3)])
                P.add('act', lambda e, b=b, c3=c3: e.activation(out=sqT[:, c3, 0:ntok], in_=ps[b][:, 0:ntok], func=AF.Square),
                      r=[f'ps{b}'], w=[('sqT', c3)])
            bk = nb()
            proj_fm(tb, 32, ntok, bk, lambda k: wkr[:, k, 0, :])
            P.add('act', lambda e: e.activation(out=kt1[:, 0:ntok], in_=ps[bk][0:32, 0:ntok], func=AF.Copy), r=[f'ps{bk}'], w=['kt1'])
            bkr = nb()
            proj_fm(tb, 32, ntok, bkr, lambda k: wkr[:, k, 1, :])
            P.add('act', lambda e: e.activation(out=kt2[:, 0:ntok], in_=ps[bkr][0:32, 0:ntok], func=AF.Copy), r=[f'ps{bkr}'], w=['kt2'])
            P.add('dve', lambda e: e.tensor_tensor(kt1[:, 0:ntok], kt1[:, 0:ntok], ropet[tb][0:32, 0, 0:ntok], ALU.mult), r=['kt1', ('ropet', tb)], w=['kt1'])
            P.add('dve', lambda e: e.tensor_tensor(kt2[:, 0:ntok], kt2[:, 0:ntok], ropet[tb][0:32, 1, 0:ntok], ALU.mult), r=['kt2', ('ropet', tb)], w=['kt2'])
            P.add('dve', lambda e: e.tensor_tensor(kpe[:, 0:ntok], kt1[:, 0:ntok], kt2[:, 0:ntok], ALU.add), r=['kt1', 'kt2'], w=['kpe'])
            for h in range(4):
                P.add('sp', lambda e, h=h: e.dma_start(out=ksc[s, h, 64:96, t0:t0 + ntok], in_=kpe[:, 0:ntok]), r=['kpe'], chan=stchan())
            if full:
                b = nb()
                def fq(e, b=b):
                    e.matmul(ps[b][:, 0:ntok], ones[:, :], sqT[:, 0, 0:ntok], start=True, stop=False)
                    e.matmul(ps[b][:, 0:ntok], ones[:, :], sqT[:, 1, 0:ntok], start=False, stop=True)
                P.add('pe', fq, r=[('sqT', 0), ('sqT', 1), 'ones'], w=[f'ps{b}'])
                act_chain(rqbc[:, 0:ntok], ps[b][:, 0:ntok], 1.0 / 256.0, f'ps{b}', 'rqbc')
                P.add('dve', lambda e: e.tensor_tensor(cqn[:, 0:2, 0:ntok], cqT[:, 0:2, 0:ntok], rqbc[:, 0:ntok].unsqueeze(1).to_broadcast([128, 2, ntok]), ALU.mult),
                      r=[('cqT', 0), ('cqT', 1), 'rqbc'], w=[('cqn', 0), ('cqn', 1)])
            b = nb()
            P.add('pe', lambda e, b=b: e.matmul(ps[b][:, 0:ntok], ones[:, :], sqT[:, 2, 0:ntok], start=True, stop=True),
                  r=[('sqT', 2), 'ones'], w=[f'ps{b}'])
            act_chain(rkvbc[:, 0:ntok], ps[b][:, 0:ntok], 1.0 / 128.0, f'ps{b}', 'rkvbc')
            P.add('dve', lambda e: e.tensor_tensor(cqn[:, 2, 0:ntok], cqT[:, 2, 0:ntok], rkvbc[:, 0:ntok], ALU.mult),
                  r=[('cqT', 2), 'rkvbc'], w=[('cqn', 2)])
            if full:
                for cc in range(2):
                    b = nb()
                    proj_fm(tb, 128, ntok, b, wcols(cc * 128))
                    P.add('act', lambda e, b=b, cc=cc: e.activation(out=uFt[pb][:, cc, 0:ntok], in_=ps[b][:, 0:ntok], func=AF.Copy),
                          r=[f'ps{b}'], w=[('uFt', pb, cc)])
                P.add('sp', lambda e: e.dma_start(out=ufsc[s, :, :, t0:t0 + ntok].rearrange("c p t -> p c t"), in_=uFt[pb][:, :, 0:ntok]),
                      r=[('uFt', pb, 0), ('uFt', pb, 1)], chan=stchan())
                for (buf, key) in ((bgb, 'bgb'), (zb, 'zb'), (ub, 'ub')):
                    if first:
                        P.add('pool', lambda e, buf=buf: e.memset(buf[pb][:, :, 0:16], 0.0), w=[(key, pb, 'c')])
                    else:
                        P.add('pool', lambda e, buf=buf: e.tensor_copy(buf[pb][:, :, 0:16], buf[pp][:, :, 512:528]),
                              r=[(key, pp, 0), (key, pp, 1)], w=[(key, pb, 'c')])
                    if lastt:
                        P.add('pool', lambda e, buf=buf: e.memset(buf[pb][:, :, 16 + ntok:24 + ntok], 0.0), w=[(key, pb, 't')])
                for cc in range(2):
                    b = nb()
                    proj_fm(tb, 128, ntok, b, wcols(256 + cc * 128))
                    P.add('act', lambda e, b=b, cc=cc: e.activation(out=bgb[pb][:, cc, 16:16 + ntok], in_=ps[b][:, 0:ntok], func=AF.Copy),
                          r=[f'ps{b}'], w=[('bgb', pb, cc)])
                    b1 = nb()
                    proj_fm(tb, 128, ntok, b1, wcols(512 + cc * 128))
                    P.add('act', lambda e, b1=b1, cc=cc: e.activation(out=cgt[:, cc, 0:ntok], in_=ps[b1][:, 0:ntok], func=AF.Copy),
                          r=[f'ps{b1}'], w=[('cgt', cc)])
                    b2 = nb()
                    proj_fm(tb, 128, ntok, b2, wcols(768 + cc * 128))
                    P.add('act', lambda e, b2=b2, cc=cc: e.activation(out=zb[pb][:, cc, 16:16 + ntok], in_=ps[b2][:, 0:ntok], func=AF.Copy),
                          r=[f'ps{b2}'], w=[('zb', pb, cc)])
                    P.add('dve', lambda e, cc=cc: e.tensor_tensor(zb[pb][:, cc, 16:16 + ntok], zb[pb][:, cc, 16:16 + ntok], cgt[:, cc, 0:ntok], ALU.mult),
                          r=[('zb', pb, cc), ('cgt', cc)], w=[('zb', pb, cc)])
                for cc in range(2):
                    b = nb()
                    proj_fm(tb, 128, ntok, b, wcols(1024 + cc * 128))
                    P.add('act', lambda e, b=b, cc=cc: e.activation(out=ub[pb][:, cc, 16:16 + ntok], in_=ps[b][:, 0:ntok], func=AF.Copy),
                          r=[f'ps{b}'], w=[('ub', pb, cc)])
            hg = mid_hook() if mid_hook is not None else iter(())
            def adv(n=1):
                for _ in range(n):
                    next(hg, None)
            adv(2)
            for cp in range(2):
                b = nb()
                P.add('pe', lambda e, b=b, cp=cp: e.matmul(ps[b][:, 0:ntok], wukvn[:, cp * 128:(cp + 1) * 128], cqn[:, 2, 0:ntok], start=True, stop=True),
                      r=['wukvn', ('cqn', 2)], w=[f'ps{b}'])
                P.add('act', lambda e, b=b, cp=cp: e.activation(out=kn[:, cp, 0:ntok], in_=ps[b][:, 0:ntok], func=AF.Copy), r=[f'ps{b}'], w=[('kn', cp)])
                adv(1)
                for hh in range(2):
                    P.add('sp', lambda e, cp=cp, hh=hh: e.dma_start(out=ksc[s, 2 * cp + hh, 0:64, t0:t0 + ntok], in_=kn[hh * 64:(hh + 1) * 64, cp, 0:ntok]),
                          r=[('kn', cp)], chan=stchan())
            for j in range(nsub):
                b = nb()
                P.add('pe', lambda e, b=b, j=j: e.matmul(ps[b][:, 0:256], cqn[:, 2, j * 128:(j + 1) * 128], wukvv[:, :], start=True, stop=True),
                      r=['wukvv', ('cqn', 2)], w=[f'ps{b}'])
                P.add('act', lambda e, b=b, j=j: e.activation(
                    out=vo[:, j, :].rearrange("p (g b d) -> p g b d", g=2, b=3)[:, :, 0:3:2, :],
                    in_=ps[b][:, 0:256].rearrange("p (g i d) -> p g i d", g=2, i=2), func=AF.Copy), r=[f'ps{b}'], w=['vo'])
                adv(1)
            P.add('sp', lambda e: e.dma_start(out=vsc[s, t0:t0 + ntok, :].rearrange("(j p) c -> p j c", p=128), in_=vo[:, 0:nsub, :]),
                  r=['vo'], chan=stchan())
            if full:
                for cp in range(2):
                    b = nb()
                    def fqn(e, b=b, cp=cp):
                        for kc in range(2):
                            e.matmul(ps[b][:, 0:ntok], wuqn[:, kc, cp * 128:(cp + 1) * 128], cqn[:, kc, 0:ntok], start=(kc == 0), stop=(kc == 1))
                    P.add('pe', fqn, r=['wuqn', ('cqn', 0), ('cqn', 1)], w=[f'ps{b}'])
                    P.add('act', lambda e, b=b, cp=cp: e.activation(out=qn[:, cp, 0:ntok], in_=ps[b][:, 0:ntok], func=AF.Copy), r=[f'ps{b}'], w=[('qn', cp)])
                    adv(1)
                    for hh in range(2):
                        P.add('sp', lambda e, cp=cp, hh=hh: e.dma_start(out=qsc[s, 2 * cp + hh, 0:64, t0:t0 + ntok], in_=qn[hh * 64:(hh + 1) * 64, cp, 0:ntok]),
                              r=[('qn', cp)], chan=stchan())
                bq = nb()
                def fqp(e, bq=bq):
                    for kc in range(2):
                        e.matmul(ps[bq][:, 0:ntok], wuqp[:, kc, :], cqn[:, kc, 0:ntok], start=(kc == 0), stop=(kc == 1))
                P.add('pe', fqp, r=['wuqp', ('cqn', 0), ('cqn', 1)], w=[f'ps{bq}'])
                P.add('act', lambda e: e.activation(out=rt1[:, 0:ntok], in_=ps[bq][:, 0:ntok], func=AF.Copy), r=[f'ps{bq}'], w=['rt1'])
                br = nb()
                def fqr(e, br=br):
                    for kc in range(2):
                        e.matmul(ps[br][:, 0:ntok], wuqr[:, kc, :], cqn[:, kc, 0:ntok], start=(kc == 0), stop=(kc == 1))
                P.add('pe', fqr, r=['wuqr', ('cqn', 0), ('cqn', 1)], w=[f'ps{br}'])
                P.add('act', lambda e: e.activation(out=rt2[:, 0:ntok], in_=ps[br][:, 0:ntok], func=AF.Copy), r=[f'ps{br}'], w=['rt2'])
                P.add('dve', lambda e: e.tensor_tensor(rt1[:, 0:ntok], rt1[:, 0:ntok], ropet[tb][:, 0, 0:ntok], ALU.mult), r=['rt1', ('ropet', tb)], w=['rt1'])
                P.add('dve', lambda e: e.tensor_tensor(rt2[:, 0:ntok], rt2[:, 0:ntok], ropet[tb][:, 1, 0:ntok], ALU.mult), r=['rt2', ('ropet', tb)], w=['rt2'])
                P.add('dve', lambda e: e.tensor_tensor(qpe[:, 0:ntok], rt1[:, 0:ntok], rt2[:, 0:ntok], ALU.add), r=['rt1', 'rt2'], w=['qpe'])
                for h in range(4):
                    P.add('sp', lambda e, h=h: e.dma_start(out=qsc[s, h, 64:96, t0:t0 + ntok], in_=qpe[h * 32:(h + 1) * 32, 0:ntok]), r=['qpe'], chan=stchan())
            def convpool():
                lo = 16 if first else 8
                hi = 16 + ntok if lastt else 8 + ntok
                n = hi - lo
                W_ = 24 + ntok if lastt else 16 + ntok
                zkeys = [('zb', pb, 0), ('zb', pb, 1), ('zb', pb, 'c'), ('zb', pb, 't')]
                bkeys = [('bgb', pb, 0), ('bgb', pb, 1), ('bgb', pb, 'c'), ('bgb', pb, 't')]
                ukeys = [('ub', pb, 0), ('ub', pb, 1), ('ub', pb, 'c'), ('ub', pb, 't')]
                for cc in range(2):
                    cw = lambda tap, cc=cc: V(f'convw{l}', 1, tap * 2 + cc)
                    P.add('dve', lambda e, cc=cc, cw=cw: e.tensor_scalar(ytmp[:, 0:n], zb[pb][:, cc, lo:hi], cw(1), None, ALU.mult),
                          r=zkeys + ['vecs'], w=['ytmp'])
                    for (tap, sh) in ((0, -1), (2, 1)):
                        P.add('dve', lambda e, cc=cc, cw=cw, tap=tap, sh=sh: e.scalar_tensor_tensor(ytmp[:, 0:n], zb[pb][:, cc, lo + sh:hi + sh], cw(tap), ytmp[:, 0:n], ALU.mult, ALU.add),
                              r=zkeys + ['vecs', 'ytmp'], w=['ytmp'])
                    P.add('dve', lambda e, cc=cc: e.tensor_tensor(sout[:, cc, 0:n], bgb[pb][:, cc, lo:hi], ytmp[:, 0:n], ALU.mult),
                          r=bkeys + ['ytmp'], w=['sout'])
                P.add('dve', lambda e: e.tensor_tensor(a2[:, :, 1:W_], ub[pb][:, :, 0:W_ - 1], ub[pb][:, :, 1:W_], ALU.add), r=ukeys, w=['a2'])
                P.add('dve', lambda e: e.tensor_tensor(a4[:, :, 2:W_ - 1], a2[:, :, 1:W_ - 2], a2[:, :, 3:W_], ALU.add), r=['a2'], w=['a4'])
                P.add('dve', lambda e: e.tensor_tensor(a8[:, 4:W_ - 3], a4[:, 1, 2:W_ - 5], a4[:, 1, 6:W_ - 1], ALU.add), r=['a4'], w=['a8'])
                P.add('dve', lambda e: e.tensor_tensor(a16[64:128, 8:W_ - 7], a8[64:128, 4:W_ - 11], a8[64:128, 12:W_ - 3], ALU.add), r=['a8'], w=['a16'])
                sels = [(0, 0, 64, a2[0:64, 0, :]), (0, 64, 128, a4[64:128, 0, :]), (1, 0, 64, a8[0:64, :]), (1, 64, 128, a16[64:128, :])]
                si = 1 if is_ctx else 0
                for (cc, p0, p1, sel) in sels:
                    P.add('dve', lambda e, cc=cc, p0=p0, p1=p1, sel=sel: e.scalar_tensor_tensor(
                        pout[p0:p1, cc, 0:n], sel[:, lo:hi], V('invw', 1, cc)[p0:p1, :], ub[pb][p0:p1, cc, lo:hi], ALU.mult, ALU.subtract),
                        r=ukeys + ['a2', 'a4', 'a8', 'a16', 'vecs'], w=['pout'])
                    pe0 = cc * 32 + si * 16
                    if first:
                        P.add('pool', lambda e, p0=p0, p1=p1, sel=sel, pe0=pe0: e.tensor_tensor(
                            etmp[p0:p1, :], sel[:, 16:24], pedge[p0:p1, pe0:pe0 + 8], ALU.mult), r=['a2', 'a4', 'a8', 'a16', 'pedge'], w=['etmp'])
                        P.add('pool', lambda e, cc=cc, p0=p0, p1=p1: e.tensor_tensor(
                            pout[p0:p1, cc, 16 - lo:24 - lo], etmp[p0:p1, :], ub[pb][p0:p1, cc, 16:24], ALU.subtract),
                            r=ukeys + ['etmp', 'pout'], w=['pout'])
                    if lastt:
                        P.add('pool', lambda e, p0=p0, p1=p1, sel=sel, pe0=pe0: e.tensor_tensor(
                            etmp[p0:p1, :], sel[:, 8 + ntok:16 + ntok], pedge[p0:p1, pe0 + 8:pe0 + 16], ALU.mult), r=['a2', 'a4', 'a8', 'a16', 'pedge'], w=['etmp'])
                        P.add('pool', lambda e, cc=cc, p0=p0, p1=p1: e.tensor_tensor(
                            pout[p0:p1, cc, 8 + ntok - lo:16 + ntok - lo], etmp[p0:p1, :], ub[pb][p0:p1, cc, 8 + ntok:16 + ntok], ALU.subtract),
                            r=ukeys + ['etmp', 'pout'], w=['pout'])
                tok_lo = t0 + (lo - 16)
                P.add('sp', lambda e: e.dma_start(out=gsc[s, 4:6, :, tok_lo:tok_lo + n].rearrange("c p t -> p c t"), in_=sout[:, :, 0:n]),
                      r=['sout'], chan=stchan())
                P.add('sp', lambda e: e.dma_start(out=gsc[s, 6:8, :, tok_lo:tok_lo + n].rearrange("c p t -> p c t"), in_=pout[:, :, 0:n]),
                      r=['pout'], chan=stchan())
            adv(1000)
            return convpool if full else None

        tiles = [(s, ti) for s in range(int(os.environ.get('DBG_NS', '2'))) for ti in range(int(os.environ.get('DBG_NT', '9')))]
        prev_cp = None
        NS = {}
        def getN(i_):
            if i_ < len(tiles) and i_ not in NS:
                NS[i_] = stageN(*tiles[i_])
            return NS.get(i_)
        n0 = getN(0)
        n0['loads'](); n0['rope']()
        for _ in n0['compute']():
            pass
        for _ in n0['part_b']():
            pass
        n1 = getN(1)
        if n1 is not None:
            n1['loads']()
            for _ in n1['compute']():
                pass
        for i_, tl in enumerate(tiles):
            n_next = getN(i_ + 1)
            n_next2 = getN(i_ + 2)
            if n_next2 is not None:
                n_next2['loads']()
            def hook(n_next=n_next, n_next2=n_next2):
                if n_next is not None:
                    for _ in n_next['part_b'](gen=True):
                        yield
                    n_next['rope']()
                if n_next2 is not None:
                    for _ in n_next2['compute'](gen=True):
                        yield
            cpf = stageP(tl[0], tl[1], NS[i_]['tb'], hook)
            if prev_cp is not None:
                prev_cp()
            prev_cp = cpf
        if prev_cp is not None:
            prev_cp()
        if debug and l == 0 and not os.environ.get('DBG_NODUMP'):
            P.barrier()
            d_uF = dbg_out("uF", [2, 128, TOK], BF16)
            for c_ in range(2):
                P.add('sp', lambda e, c_=c_: e.dma_start(out=d_uF[c_], in_=ufsc[0, c_]), chan='st1')
            for nm, src_, nch in (("gsc", gsc, 8), ("qsc", qsc, 4), ("ksc", ksc, 4)):
                dd = dbg_out(nm, list(src_.shape[1:]), BF16)
                for c_ in range(nch):
                    P.add('sp', lambda e, dd=dd, src_=src_, c_=c_: e.dma_start(out=dd[c_], in_=src_[0, c_]), chan='st1')
            dd = dbg_out("vsc", [TOK, 384], BF16)
            for c_ in range(4):
                P.add('sp', lambda e, dd=dd, c_=c_: e.dma_start(out=dd[c_ * 1088:(c_ + 1) * 1088, :], in_=vsc[0, c_ * 1088:(c_ + 1) * 1088, :]), chan='st1')
        P.barrier()
        A.pop()
        if stop_after == ('A', l):
            break

        A.push()
        uF = A.alloc("uF", [128, 2, SEQ], BF16)
        u2 = A.alloc("u2", [64, 64, 256], BF16)
        Acc = A.alloc("Acc", [64, 128, 128], BF16)
        M2b = A.alloc("M2b", [64, 2, 8192], BF16)
        W1b = A.alloc("W1b", [64, 128], BF16)
        Xo = A.alloc("Xo", [128, 2, SEQ], BF16)
        uc = A.alloc("uc", [128, 2, CTXL], BF16)
        uct = A.alloc("uct", [128, 2, 256], BF16)
        d256 = A.alloc("d256", [128, 2, 512], BF16)
        xc = A.alloc("xc", [128, 2, 2, CTXL], BF16)
        P.add('pool', lambda e: e.dma_start(out=W1b[:], in_=c_W1), w=['W1b'], chan='wa')
        for c_ in range(2):
            P.add('pool', lambda e, c_=c_: e.dma_start(out=M2b[:, c_, :], in_=c_M2[c_]), w=[('M2b', c_)], chan='wb' if c_ else 'wa')
        P.add('pool', lambda e: e.dma_start(out=d256[:], in_=c_D256.rearrange("(c p) f -> p c f", p=128)), w=['d256'], chan='wa')
        fb = {'n': 0}

        def fbank():
            b = fb['n'] % 8
            fb['n'] += 1
            return b
        for s in range(int(os.environ.get('DBG_NS', '2'))):
            P.add('sp', lambda e: e.dma_start(out=uF[:], in_=ufsc[s, :, :, CTXL:TOK].rearrange("c p t -> p c t")), r=[('ufsc', s)], w=['uF'], chan='ld0')
            for g4 in range(16):
                b = fbank()
                def t1(e, g4=g4, b=b):
                    for i4 in range(4):
                        n2 = g4 * 4 + i4
                        for cc in range(2):
                            e.transpose(psb16[b][0:64, i4 * 256 + cc * 128: i4 * 256 + (cc + 1) * 128],
                                        uF[:, cc, :].rearrange("p (a n) -> p n a", n=64)[:, n2, :], ident[:])
                P.add('pe', t1, r=['uF', 'ident'], w=[f'ps{b}'])
                if g4 % 2 == 0:
                    P.add('act', lambda e, g4=g4, b=b: e.activation(out=u2[:, g4 * 4:(g4 + 1) * 4, :], in_=psb16[b][0:64, :].rearrange("p (i c) -> p i c", i=4), func=AF.Copy),
                          r=[f'ps{b}'], w=['u2'])
                else:
                    P.add('dve', lambda e, g4=g4, b=b: e.tensor_copy(u2[:, g4 * 4:(g4 + 1) * 4, :], psb16[b][0:64, :].rearrange("p (i c) -> p i c", i=4)),
                          r=[f'ps{b}'], w=['u2'], free_psum=True)
            for cc in range(2):
                for g4 in range(32):
                    b = fbank()
                    def s1(e, g4=g4, b=b, cc=cc):
                        for i4 in range(4):
                            ch = cc * 128 + g4 * 4 + i4
                            e.matmul(ps[b][0:64, i4 * 128:(i4 + 1) * 128], u2[:, :, ch], W1b[:, :], start=True, stop=True)
                    P.add('pe', s1, r=['u2', 'W1b'], w=[f'ps{b}'])
                    if g4 % 2 == 0:
                        P.add('act', lambda e, g4=g4, b=b: e.activation(out=Acc[:, g4 * 4:(g4 + 1) * 4, :], in_=ps[b][0:64, :].rearrange("p (i c) -> p i c", i=4), func=AF.Copy),
                              r=[f'ps{b}'], w=['Acc'])
                    else:
                        P.add('dve', lambda e, g4=g4, b=b: e.tensor_copy(Acc[:, g4 * 4:(g4 + 1) * 4, :], ps[b][0:64, :].rearrange("p (i c) -> p i c", i=4)),
                              r=[f'ps{b}'], w=['Acc'], free_psum=True)
                for g4 in range(16):
                    b = fbank()
                    def s2(e, g4=g4, b=b):
                        for i4 in range(4):
                            k1 = g4 * 4 + i4
                            for c_ in range(2):
                                e.matmul(ps[b][:, i4 * 128:(i4 + 1) * 128], Acc[:, :, c_ * 64 + k1], M2b[:, c_, k1 * 128:(k1 + 1) * 128],
                                         start=(c_ == 0), stop=(c_ == 1))
                    P.add('pe', s2, r=['Acc', ('M2b', 0), ('M2b', 1)], w=[f'ps{b}'])
                    if g4 % 2 == 0:
                        P.add('act', lambda e, g4=g4, b=b: e.activation(
                            out=Xo[:, :, :].rearrange("p c (k2 k1) -> p k1 c k2", k1=64)[:, g4 * 4:(g4 + 1) * 4, :, :],
                            in_=ps[b][:, :].rearrange("p (i c k) -> p i c k", i=4, c=2), func=AF.Copy), r=[f'ps{b}'], w=['Xo'])
                    else:
                        P.add('dve', lambda e, g4=g4, b=b: e.tensor_copy(
                            Xo[:, :, :].rearrange("p c (k2 k1) -> p k1 c k2", k1=64)[:, g4 * 4:(g4 + 1) * 4, :, :],
                            ps[b][:, :].rearrange("p (i c k) -> p i c k", i=4, c=2)), r=[f'ps{b}'], w=['Xo'], free_psum=True)
                P.add('sp', lambda e, cc=cc: e.dma_start(out=gsc[s, cc:cc + 3:2, :, CTXL:TOK].rearrange("c p t -> p c t"), in_=Xo[:, :, :]),
                      r=['Xo'], chan='st0')
            if not last_layer:
                P.add('sp', lambda e: e.dma_start(out=uc[:], in_=ufsc[s, :, :, 0:CTXL].rearrange("c p t -> p c t")), r=[('ufsc', s)], w=['uc'], chan='ld1')
                b = fbank()
                def tc(e, b=b):
                    for nc_ in range(2):
                        for cc in range(2):
                            e.transpose(psb16[b][:, nc_ * 256 + cc * 128: nc_ * 256 + (cc + 1) * 128], uc[:, cc, nc_ * 128:(nc_ + 1) * 128], ident[:])
                P.add('pe', tc, r=['uc', 'ident'], w=[f'ps{b}'])
                P.add('act', lambda e, b=b: e.activation(out=uct[:, :, :], in_=psb16[b][:, 0:512].rearrange("p (n c) -> p n c", n=2), func=AF.Copy), r=[f'ps{b}'], w=['uct'])
                for cc in range(2):
                    b = fbank()
                    def dc(e, b=b, cc=cc):
                        for nc_ in range(2):
                            e.matmul(ps[b][:, :], uct[:, nc_, cc * 128:(cc + 1) * 128], d256[:, nc_, :], start=(nc_ == 0), stop=(nc_ == 1))
                    P.add('pe', dc, r=['uct', 'd256'], w=[f'ps{b}'])
                    P.add('act', lambda e, b=b, cc=cc: e.activation(out=xc[:, cc, :, :], in_=ps[b][:, :].rearrange("p (c k) -> p c k", c=2), func=AF.Copy), r=[f'ps{b}'], w=[('xc', cc)])
                for cc in range(2):
                    P.add('sp', lambda e, cc=cc: e.dma_start(out=gsc[s, cc:cc + 3:2, :, 0:CTXL].rearrange("c p t -> p c t"), in_=xc[:, cc, :, :]),
                          r=[('xc', cc)], chan='st1')
        if debug and l == 0 and os.environ.get('DBG_DUMPF'):
            P.barrier()
            dd = dbg_out("gsc", [8, 128, TOK], BF16)
            for c_ in range(8):
                P.add('sp', lambda e, dd=dd, c_=c_: e.dma_start(out=dd[c_], in_=gsc[0, c_]), chan='st1')
        P.barrier()
        A.pop()
        if stop_after == ('F', l):
            break

        A.push()
        weffc = A.alloc("weffc", [128, 10, D], BF16)
        g1bc = A.alloc("g1bc", [128, D], F32)
        KT = A.alloc("KT", [96, 4, TOK], BF16)
        Vg = A.alloc("Vg", [128, 34, 384], BF16)
        Qt = [A.alloc("Qt", [96, 4, 512], BF16) for _ in range(2)]
        Gt = A.alloc("Gt", [128, 8, 512], BF16)
        xtb = A.alloc("xtb", [128, 4, D], F32)
        attnT = [A.alloc("attnT", [128, 2, 512], BF16) for _ in range(2)]
        ptb = [A.alloc("ptb", [128, 2, 512], BF16) for _ in range(3)]
        rcb = A.alloc("rcb", [128, 512], F32)
        accs = [A.alloc("accs", [128, 512], F32) for _ in range(2)]
        wtmp = A.alloc("wtmp", [128, D], F32)
        VCOL = [0, 64, 192, 256]
        ctr = {'pt': 0, 'sg': 0}
        b1tiles = []
        for s in range(int(os.environ.get('DBG_NS', '2'))):
            for ti in range(9):
                if ti == 0 and last_layer:
                    continue
                b1tiles.append((s, ti))

        def b1info(s, ti):
            is_ctx = (ti == 0)
            ntok = CTXL if is_ctx else 512
            t0 = 0 if is_ctx else CTXL + (ti - 1) * 512
            return is_ctx, ntok, t0

        def load_q(idx):
            s, ti = b1tiles[idx]
            is_ctx, ntok, t0 = b1info(s, ti)
            P.add('sp', lambda e: e.dma_start(out=Qt[idx % 2][:, :, 0:ntok], in_=qsc[s, :, :, t0:t0 + ntok].rearrange("h p t -> p h t")),
                  r=[('qsc', s)], w=[('Qt', idx % 2)], chan='ld3')

        def load_gx(idx):
            s, ti = b1tiles[idx]
            is_ctx, ntok, t0 = b1info(s, ti)
            P.add('sp', lambda e: e.dma_start(out=Gt[:, :, 0:ntok], in_=gsc[s, :, :, t0:t0 + ntok].rearrange("c p t -> p c t")),
                  r=[('gsc', s)], w=['Gt'], chan='ld4')
            for j in range(ntok // 128):
                P.add('sp', lambda e, j=j: e.dma_start(out=xtb[:, j, :], in_=src_rows(l, s, t0 + j * 128, 128)),
                      w=[('xtb', j)], chan=f'xl{j % 2}')

        def wout_chunks(idx):
            s, ti = b1tiles[idx]
            is_ctx, ntok, t0 = b1info(s, ti)
            nsub = ntok // 128
            ab = idx % 2
            for j in range(nsub):
                for half in range(2):
                    def wo(e, j=j, half=half):
                        for c_ in range(10):
                            lh = Gt[:, c_, j * 128:(j + 1) * 128] if c_ < 8 else attnT[ab][:, c_ - 8, j * 128:(j + 1) * 128]
                            e.matmul(psg[3][:, half * 512:(half + 1) * 512], lh, weffc[:, c_, half * 512:(half + 1) * 512], start=(c_ == 0), stop=(c_ == 9))
                    P.add('pe', wo, r=['Gt'] + [('attnT', ab, a_, b_) for a_ in range(2) for b_ in range(2)] + [('weffc', c_) for c_ in range(10)], w=['psg3'])
                    yield
                P.add('dve', lambda e: e.tensor_copy(wtmp[:, :], psg[3][:, :]), r=['psg3'], w=['wtmp'], free_psum=True)
                P.add('dve', lambda e, j=j: e.tensor_tensor(xtb[:, j, :], wtmp[:, :], xtb[:, j, :], ALU.add),
                      r=['wtmp', ('xtb', j)], w=[('xtb', j)])
            P.add('sp', lambda e: e.dma_start(out=xs1[s, t0:t0 + ntok, :].rearrange("(j p) d -> p j d", p=128), in_=xtb[:, 0:nsub, :]),
                  r=[('xtb', j) for j in range(nsub)], chan='st2')
            yield

        cur = {'s': None, 'm': None}
        pending_wout = None
        load_q(0)
        for idx, (s, ti) in enumerate(b1tiles):
            is_ctx, ntok, t0 = b1info(s, ti)
            m = 2 if is_ctx else s
            nkp = 1 if is_ctx else 17
            ab_ = idx % 2
            if cur['s'] != s or cur['m'] != m:
                if pending_wout is not None:
                    for _ in pending_wout:
                        pass
                    pending_wout = None
            if cur['s'] != s:
                cur['s'] = s
                for h_ in range(4):
                    P.add('sp', lambda e, h_=h_: e.dma_start(out=KT[:, h_, :], in_=ksc[s, h_]), w=[('KT', h_)], chan=f'ldk{h_}')
                for v_ in range(2):
                    P.add('sp', lambda e, v_=v_: e.dma_start(out=Vg[:, v_ * 17:(v_ + 1) * 17, :],
                                                            in_=vsc[s, v_ * 2176:(v_ + 1) * 2176, :].rearrange("(c p) f -> p c f", p=128)),
                          w=[('Vg', v_)], chan=f'ldv{v_}')
            if cur['m'] != m:
                cur['m'] = m
                P.add('sp', lambda e, m=m: e.dma_start(out=g1bc[:], in_=modsc[l, m:m + 1, 2 * D:3 * D].to_broadcast([128, D])),
                      r=[('modsc', l)], w=['g1bc'], chan='ld2')
                for c_ in range(10):
                    P.add('dve', lambda e, c_=c_: e.tensor_tensor(weffc[:, c_, :], weff[:, c_, :], g1bc[:], ALU.mult),
                          r=[('weff', c_), 'g1bc'], w=[('weffc', c_)])
            if idx + 1 < len(b1tiles):
                load_q(idx + 1)
            tb = idx % 2
            npairs_total = 4 * nkp
            every = max(1, npairs_total // 9)
            pcount = 0
            for hp in range(2):
                accb = []
                for hh in range(2):
                    h = hp * 2 + hh
                    ab = 4 + hh
                    accb.append(ab)
                    vc = VCOL[h]

                    def do_s(kp, h=h):
                        g = ctr['sg'] % 2
                        ctr['sg'] += 1
                        def f(e, kp=kp, g=g):
                            for i2 in range(2):
                                kc = kp * 2 + i2
                                e.matmul(psg[g][:, i2 * 512:i2 * 512 + ntok], KT[0:96, h, kc * 128:(kc + 1) * 128], Qt[tb][0:96, h, 0:ntok],
                                         start=True, stop=True)
                        P.add('pe', f, r=[('KT', h), ('Qt', tb)], w=[f'psg{g}'])
                        pi = ctr['pt'] % 3
                        ctr['pt'] += 1
                        P.add('act', lambda e, g=g, pi=pi: e.activation(out=ptb[pi][:, :, 0:ntok], in_=psg[g].rearrange("p (b n) -> p b n", b=2)[:, :, 0:ntok],
                                                                       func=AF.Exp, scale=SCALE), r=[f'psg{g}'], w=[('ptb', pi)])
                        return pi

                    def do_pv(kp, pi, ab=ab, vc=vc):
                        def f(e, kp=kp, pi=pi):
                            for i2 in range(2):
                                kc = kp * 2 + i2
                                e.matmul(ps[ab][:, 0:ntok], Vg[:, kc, vc:vc + 128], ptb[pi][:, i2, 0:ntok],
                                         start=(kc == 0), stop=(kc == 2 * nkp - 1))
                        P.add('pe', f, r=[('Vg', 0 if kp * 2 + 1 < 17 else 1), ('Vg', 0 if kp * 2 < 17 else 1), ('ptb', pi)], w=[f'ps{ab}'])
                    pend = []
                    for kp in range(nkp):
                        pi = do_s(kp)
                        pend.append((kp, pi))
                        if len(pend) > 1:
                            do_pv(*pend.pop(0))
                        pcount += 1
                        if pending_wout is not None and pcount % every == 0:
                            if next(pending_wout, 'done') == 'done':
                                pending_wout = None
                    while pend:
                        do_pv(*pend.pop(0))
                ae, ao = accb
                P.add('dve', lambda e, ae=ae: e.tensor_copy(accs[0][:, 0:ntok], ps[ae][:, 0:ntok]), r=[f'ps{ae}'], w=[('accs', 0)], free_psum=True)
                P.add('dve', lambda e, ao=ao: e.tensor_copy(accs[1][:, 0:ntok], ps[ao][:, 0:ntok]), r=[f'ps{ao}'], w=[('accs', 1)], free_psum=True)
                P.add('dve', lambda e: e.reciprocal(rcb[0:64, 0:ntok], accs[0][64:128, 0:ntok]), r=[('accs', 0)], w=[('rcb', 0)])
                P.add('dve', lambda e, hp=hp: e.tensor_tensor(attnT[ab_][0:64, hp, 0:ntok], accs[0][0:64, 0:ntok], rcb[0:64, 0:ntok], ALU.mult),
                      r=[('accs', 0), ('rcb', 0)], w=[('attnT', ab_, hp, 0)])
                P.add('dve', lambda e: e.reciprocal(rcb[64:128, 0:ntok], accs[1][0:64, 0:ntok]), r=[('accs', 1)], w=[('rcb', 1)])
                P.add('dve', lambda e, hp=hp: e.tensor_tensor(attnT[ab_][64:128, hp, 0:ntok], accs[1][64:128, 0:ntok], rcb[64:128, 0:ntok], ALU.mult),
                      r=[('accs', 1), ('rcb', 1)], w=[('attnT', ab_, hp, 1)])
            if pending_wout is not None:
                for _ in pending_wout:
                    pass
            load_gx(idx)
            pending_wout = wout_chunks(idx)
        if pending_wout is not None:
            for _ in pending_wout:
                pass
        if debug and l == 0 and os.environ.get('DBG_DUMPB1'):
            P.barrier()
            dd = dbg_out("xs1", [TOK, D], F32)
            for c_ in range(4):
                P.add('sp', lambda e, dd=dd, c_=c_: e.dma_start(out=dd[c_ * 1088:(c_ + 1) * 1088, :], in_=xs1[0, c_ * 1088:(c_ + 1) * 1088, :]), chan='st1')
        P.barrier()
        A.pop()
        if stop_after == ('B1', l):
            break

        A.pop()
        A.push()
        w1b = A.alloc("w1b", [128, 8, 4 * D], BF16)
        w2b = A.alloc("w2b", [128, 32, D], BF16)
        nb2 = dict(
            xt=[A.alloc("xs2b", [128, D], F32) for _ in range(2)],
            xn=[A.alloc("xn2", [128, D], BF16) for _ in range(2)],
            hT=[A.alloc("hT2", [128, 8, 512], BF16)],
            ssq=[A.alloc("ssq2", [128, 4], F32)],
            junk=None,
        )
        xr = [A.alloc("xr2", [128, D], F32) for _ in range(2)]
        uT = A.alloc("uT", [128, 32, 512], BF16)
        rtb = [A.alloc("rtb", [128, 512], F32) for _ in range(2)]
        g2bc = A.alloc("g2bc", [128, D], BF16)
        fngbc = A.alloc("fngbc", [128, D], F32)
        ssf = A.alloc("ssf", [128, 4], F32)
        for k in range(8):
            P.add('pool', lambda e, k=k: e.dma_start(out=w1b[:, k, :], in_=mlp_w1[l, k * 128:(k + 1) * 128, :]), w=[('w1b', k)],
                  chan='wa' if k % 2 == 0 else 'wb')
        for j4 in range(8):
            P.add('pool', lambda e, j4=j4: e.dma_start(out=w2b[:, j4 * 4:(j4 + 1) * 4, :],
                                                      in_=mlp_w2[l, j4 * 512:(j4 + 1) * 512, :].rearrange("(j p) d -> p j d", p=128)),
                  w=[('w2b', j4)], chan='wa' if j4 % 2 == 0 else 'wb')
        if last_layer:
            P.add('sp', lambda e: e.dma_start(out=fngbc[:], in_=fng_in.to_broadcast([128, D])), w=['fngbc'], chan='ld2')
        ctr2 = {'r': 0}
        b2tiles = []
        for s in range(int(os.environ.get('DBG_NS', '2'))):
            for ti in range(9):
                if ti == 0 and last_layer:
                    continue
                b2tiles.append((s, ti))

        def b2info(s, ti):
            is_ctx = (ti == 0)
            ntok = CTXL if is_ctx else 512
            t0 = 0 if is_ctx else CTXL + (ti - 1) * 512
            return is_ctx, ntok, t0, (2 if is_ctx else s)

        def b2norm(idx):
            s, ti = b2tiles[idx]
            is_ctx, ntok, t0, m = b2info(s, ti)
            return norm_stage(nb2, l, s, t0, ntok, scB[l], modT[l][:, 72:96], m, lambda l_, s_, t_, n_: xs1[s_, t_:t_ + n_, :], split='stream')

        NB = {0: b2norm(0)}
        NB[0][1]()
        NB[0][2]()
        cur_m = None
        for idx, (s, ti) in enumerate(b2tiles):
            is_ctx, ntok, t0, m = b2info(s, ti)
            nsub = ntok // 128
            if cur_m != m:
                cur_m = m
                P.add('pool', lambda e, m=m: e.dma_start(out=g2bc[:], in_=modsc[l, m:m + 1, 5 * D:6 * D].to_broadcast([128, D])),
                      r=[('modsc', l)], w=['g2bc'], chan='wa')
            for jf in range(32):
                b = nb()
                def m1(e, jf=jf, b=b):
                    for k in range(8):
                        e.matmul(ps[b][:, 0:ntok], w1b[:, k, jf * 128:(jf + 1) * 128], nb2['hT'][0][:, k, 0:ntok], start=(k == 0), stop=(k == 7))
                P.add('pe', m1, r=[('w1b', k) for k in range(8)] + [('hT', 0, k) for k in range(8)], w=[f'ps{b}'])
                ri = ctr2['r'] % 2
                ctr2['r'] += 1
                P.add('act', lambda e, b=b, ri=ri: e.activation(out=rtb[ri][:, 0:ntok], in_=ps[b][:, 0:ntok], func=AF.Relu),
                      r=[f'ps{b}'], w=[('rtb', ri)])
                P.add('pool' if jf % 4 == 0 else 'dve', lambda e, jf=jf, ri=ri: e.tensor_tensor(uT[:, jf, 0:ntok], rtb[ri][:, 0:ntok], rtb[ri][:, 0:ntok], ALU.mult),
                      r=[('rtb', ri)], w=[('uT', jf)])
            if idx + 1 < len(b2tiles):
                NB[idx + 1] = b2norm(idx + 1)
                NB[idx + 1][1]()
            for j in range(nsub):
                xj = xr[j % 2]
                P.add('sp', lambda e, j=j, xj=xj: e.dma_start(out=xj[:, :], in_=xs1[s, t0 + j * 128:t0 + (j + 1) * 128, :]), w=[('xr', j % 2)], chan=f'xr{j % 2}')
                for half in range(2):
                    ob = nb()
                    def m2(e, j=j, half=half, ob=ob):
                        for jf in range(32):
                            e.matmul(ps[ob][:, :], uT[:, jf, j * 128:(j + 1) * 128], w2b[:, jf, half * 512:(half + 1) * 512], start=(jf == 0), stop=(jf == 31))
                    P.add('pe', m2, r=[('uT', jf) for jf in range(32)] + [('w2b', j4) for j4 in range(8)], w=[f'ps{ob}'])
                    ri = ctr2['r'] % 2
                    ctr2['r'] += 1
                    P.add('dve', lambda e, ob=ob, ri=ri: e.tensor_copy(rtb[ri][:, :], ps[ob][:, :]), r=[f'ps{ob}'], w=[('rtb', ri)], free_psum=True)
                    P.add('dve', lambda e, half=half, ri=ri: e.tensor_tensor(rtb[ri][:, :], rtb[ri][:, :], g2bc[:, half * 512:(half + 1) * 512], ALU.mult),
                          r=[('rtb', ri), 'g2bc'], w=[('rtb', ri)])
                    P.add('dve', lambda e, xj=xj, half=half, ri=ri, j=j: e.tensor_tensor(xj[:, half * 512:(half + 1) * 512], rtb[ri][:, :], xj[:, half * 512:(half + 1) * 512], ALU.add),
                          r=[('rtb', ri), ('xr', j % 2)], w=[('xr', j % 2)])
                if j == 0 and idx + 1 < len(b2tiles):
                    NB[idx + 1][2]()
                if last_layer:
                    P.add('act', lambda e, j=j, xj=xj: e.activation(out=rtb[0][:, :].bitcast(BF16), in_=xj[:, :], func=AF.Square, accum_out=ssf[:, j:j + 1]),
                          r=[('xr', j % 2)], w=[('rtb', 0), ('ssf', j)])
                    rsqrt_chain(ssf[:, j:j + 1], ssf[:, j:j + 1], 1.0 / D, ('ssf', j), ('ssf', j))
                    P.add('dve', lambda e, j=j, xj=xj: e.scalar_tensor_tensor(xj[:, :], xj[:, :], ssf[:, j:j + 1], fngbc[:], ALU.mult, ALU.mult),
                          r=[('xr', j % 2), ('ssf', j), 'fngbc'], w=[('xr', j % 2)])
                    P.add('sp', lambda e, j=j, xj=xj: e.dma_start(out=y_out[s, t0 - CTXL + j * 128:t0 - CTXL + (j + 1) * 128, :], in_=xj[:, :]),
                          r=[('xr', j % 2)], chan=f'so{j % 2}')
                else:
                    P.add('sp', lambda e, j=j, xj=xj: e.dma_start(out=xs2[s, t0 + j * 128:t0 + (j + 1) * 128, :], in_=xj[:, :]),
                          r=[('xr', j % 2)], chan=f'so{j % 2}')
        if debug and l == 0 and os.environ.get('DBG_DUMPB2'):
            P.barrier()
            dd = dbg_out("xs2", [TOK, D], F32)
            for c_ in range(4):
                P.add('sp', lambda e, dd=dd, c_=c_: e.dma_start(out=dd[c_ * 1088:(c_ + 1) * 1088, :], in_=xs2[0, c_ * 1088:(c_ + 1) * 1088, :]), chan='st1')
        P.barrier()
        A.pop()
        if stop_after == ('B2', l):
            break

    P.barrier()
    P.emit()
    print('arena peak', A.peak, 'instr', {e: len(P.streams[e]) for e in ENGS})
    return nc, dbg


def prep_inputs(inputs):
    cst = _get_consts()
    f = lambda a: np.ascontiguousarray(np.asarray(a, dtype=np.float32))
    x = f(inputs['x']); c = f(inputs['c']); ctx = f(inputs['ctx']); c_ctx = f(inputs['c_ctx'])
    shared = {k: f(inputs[k]) for k in ('ada_w', 'ada_b', 'w_in', 'fourier_w', 'pool_w', 'w_uq', 'w_ukv', 'w_out',
                                        'mlp_w1', 'mlp_w2')}
    shared['fng'] = f(inputs['final_norm_g']).reshape(1, D)
    for k, v in cst.items():
        shared[k] = v
    vec_base = np.zeros((128, NVEC), np.float32)
    for l in range(DEPTH):
        vec_base[:, VEC_COLS[f'n1g{l}']:VEC_COLS[f'n1g{l}'] + 8] = _chunked(inputs['norm1_g'][l])
        vec_base[:, VEC_COLS[f'n2g{l}']:VEC_COLS[f'n2g{l}'] + 8] = _chunked(inputs['norm2_g'][l])
        vec_base[:, VEC_COLS[f'qng{l}']:VEC_COLS[f'qng{l}'] + 2] = _chunked(inputs['q_norm_g'][l])
        vec_base[:, VEC_COLS[f'kvng{l}']:VEC_COLS[f'kvng{l}'] + 1] = _chunked(inputs['kv_norm_g'][l])
        cw = np.asarray(inputs['conv_w'][l], np.float32)
        for tap in range(3):
            vec_base[:, VEC_COLS[f'convw{l}'] + tap * 2:VEC_COLS[f'convw{l}'] + tap * 2 + 2] = _chunked(cw[tap])
        vec_base[:, VEC_COLS[f'pscale{l}']:VEC_COLS[f'pscale{l}'] + 2] = _chunked(inputs['pool_scale'][l])
    invw = np.zeros((128, 2), np.float32)
    invw[:64, 0] = 1 / 2; invw[64:, 0] = 1 / 4; invw[:64, 1] = 1 / 8; invw[64:, 1] = 1 / 16
    vec_base[:, VEC_COLS['invw']:VEC_COLS['invw'] + 2] = invw
    in_maps = []
    for i in range(NCORES):
        m = dict(shared)
        m['x'] = x[2 * i:2 * i + 2]
        m['ctx'] = ctx[2 * i:2 * i + 2]
        vb = vec_base.copy()
        cm = np.stack([c[2 * i], c[2 * i + 1], c_ctx], axis=1)
        vb[:, VEC_COLS['cT']:VEC_COLS['cT'] + 24] = cm.reshape(8, 128, 3).transpose(1, 0, 2).reshape(128, 24)
        m['vecs'] = vb
        in_maps.append(m)
    return in_maps


_NC_CACHE = {}


def kernel(**inputs):
    in_maps = prep_inputs(inputs)
    if 'nc' not in _NC_CACHE:
        _NC_CACHE['nc'] = build_program()[0]
    nc = _NC_CACHE['nc']
    res = run_bass_kernel_spmd(nc, in_maps, core_ids=list(range(NCORES)))
    out = np.concatenate([np.asarray(r['y'], dtype=np.float32) for r in res.results], axis=0)
    return out
```
